# Optimizing a Trainium2 kernel written in Bass

```python
import jax, jax.numpy as jnp
from jax import lax
import numpy as np

D_MODEL = 1024
BATCH = 4
SEQ = 8192
DEPTH = 1

N_META = 16
GRID_W = 64
Q_BLOCK = 128
HEAD_DIM = 128
N_Q_HEADS = 8
N_KV_HEADS = 2
GQA_GROUP = N_Q_HEADS // N_KV_HEADS
ATTN_WIDTH = N_Q_HEADS * HEAD_DIM
KV_WIDTH = N_KV_HEADS * HEAD_DIM
POOL_WINDOWS = (2, 4, 8, 16)
N_POOL_GROUPS = len(POOL_WINDOWS)
POOL_GROUP_DIM = 128
POOL_WIDTH = N_POOL_GROUPS * POOL_GROUP_DIM
N_BRANCHES = 2
GATE_WIDTH = N_BRANCHES * D_MODEL
IN_WIDTH = ATTN_WIDTH + 2 * KV_WIDTH + POOL_WIDTH + GATE_WIDTH
D_FF = 4 * D_MODEL
ROPE_THETA = 10000.0
NORM_EPS = 1e-6

kernel_name = "hybrid_gated_gqa_axialrope_multipool_block"


def rms_norm(x, g):
    x32 = x.astype(jnp.float32)
    y = x32 * lax.rsqrt(jnp.mean(x32 * x32, axis=-1, keepdims=True) + NORM_EPS)
    return (y * g.astype(jnp.float32)).astype(x.dtype)


def axial_rope_tables(rows):
    quarter = HEAD_DIM // 4
    inv_freq = ROPE_THETA ** (-jnp.arange(quarter, dtype=jnp.float32) / quarter)
    zeros = jnp.zeros((N_META,), jnp.float32)
    row_ids = jnp.concatenate([zeros, jnp.repeat(jnp.arange(rows), GRID_W).astype(jnp.float32)])
    col_ids = jnp.concatenate([zeros, jnp.tile(jnp.arange(GRID_W), rows).astype(jnp.float32)])
    ang_r = row_ids[:, None] * inv_freq[None, :]
    ang_c = col_ids[:, None] * inv_freq[None, :]
    return jnp.cos(ang_r), jnp.sin(ang_r), jnp.cos(ang_c), jnp.sin(ang_c)


def apply_axial_rope(t, cos_r, sin_r, cos_c, sin_c):
    half, quarter = HEAD_DIM // 2, HEAD_DIM // 4
    t32 = t.astype(jnp.float32)

    def rot(xa, c, s):
        c = c[None, :, None, :]
        s = s[None, :, None, :]
        x1, x2 = xa[..., :quarter], xa[..., quarter:]
        return jnp.concatenate([x1 * c - x2 * s, x2 * c + x1 * s], axis=-1)

    out = jnp.concatenate([rot(t32[..., :half], cos_r, sin_r),
                           rot(t32[..., half:], cos_c, sin_c)], axis=-1)
    return out.astype(t.dtype)


def gqa_attention(q, k, v):
    b, l, _, dh = q.shape
    scale = 1.0 / np.sqrt(dh)
    qg = q.reshape(b, l, N_KV_HEADS, GQA_GROUP, dh).transpose(0, 2, 3, 1, 4)
    kt = k.transpose(0, 2, 1, 3)
    vt = v.transpose(0, 2, 1, 3)

    def attend(qb):
        s = jnp.einsum('bkgqd,bktd->bkgqt', qb, kt).astype(jnp.float32) * scale
        p = jax.nn.softmax(s, axis=-1)
        return jnp.einsum('bkgqt,bktd->bkgqd', p.astype(vt.dtype), vt)

    o_meta = attend(qg[:, :, :, :N_META])
    n_real = l - N_META
    nb = n_real // Q_BLOCK
    q_real = qg[:, :, :, N_META:].reshape(b, N_KV_HEADS, GQA_GROUP, nb, Q_BLOCK, dh)
    q_real = jnp.moveaxis(q_real, 3, 0)
    o_real = lax.map(attend, q_real)
    o_real = jnp.moveaxis(o_real, 0, 3).reshape(b, N_KV_HEADS, GQA_GROUP, n_real, dh)
    o = jnp.concatenate([o_meta, o_real], axis=3)
    return o.transpose(0, 3, 1, 2, 4).reshape(b, l, N_Q_HEADS * dh)


def multi_scale_pool(p):
    b, l, _ = p.shape
    t = jnp.arange(l)
    outs = []
    for g, w in enumerate(POOL_WINDOWS):
        xg = p[..., g * POOL_GROUP_DIM:(g + 1) * POOL_GROUP_DIM].astype(jnp.float32)
        c = jnp.concatenate([jnp.zeros((b, 1, POOL_GROUP_DIM), jnp.float32),
                             jnp.cumsum(xg, axis=1)], axis=1)
        lo = jnp.clip(t - w // 2, 0, l)
        hi = jnp.clip(t + w - w // 2, 0, l)
        cnt = (hi - lo).astype(jnp.float32)[None, :, None]
        mean = (c[:, hi] - c[:, lo]) / cnt
        outs.append(mean - xg)
    return jnp.concatenate(outs, axis=-1).astype(p.dtype)


def hybrid_layer(x, rope, pre_mix_g, q_norm_g, k_norm_g, w_in, w_attn_br,
                 w_pool_grp, pool_scale, w_pool_br, w_out, post_mix_g,
                 pre_mlp_g, w_mlp_in, w_mlp_out, post_mlp_g):
    b, l, _ = x.shape
    h = rms_norm(x, pre_mix_g)
    proj = h @ w_in
    s1 = ATTN_WIDTH
    s2 = s1 + KV_WIDTH
    s3 = s2 + KV_WIDTH
    s4 = s3 + POOL_WIDTH
    q, k, v, p_in, gate_logits = jnp.split(proj, [s1, s2, s3, s4], axis=-1)
    q = rms_norm(q.reshape(b, l, N_Q_HEADS, HEAD_DIM), q_norm_g)
    k = rms_norm(k.reshape(b, l, N_KV_HEADS, HEAD_DIM), k_norm_g)
    v = v.reshape(b, l, N_KV_HEADS, HEAD_DIM)
    q = apply_axial_rope(q, *rope)
    k = apply_axial_rope(k, *rope)
    attn_br = gqa_attention(q, k, v) @ w_attn_br

    pooled = multi_scale_pool(p_in).reshape(b, l, N_POOL_GROUPS, POOL_GROUP_DIM)
    pooled = jnp.einsum('blgc,gcd->blgd', pooled, w_pool_grp).reshape(b, l, POOL_WIDTH)
    pool_br = (pooled * pool_scale) @ w_pool_br

    gates = jax.nn.sigmoid(gate_logits.astype(jnp.float32)).astype(x.dtype)
    gates = gates.reshape(b, l, N_BRANCHES, D_MODEL)
    mixed = gates[:, :, 0] * attn_br + gates[:, :, 1] * pool_br
    x = x + rms_norm(mixed @ w_out, post_mix_g)

    h = rms_norm(x, pre_mlp_g)
    u = jnp.square(jax.nn.relu(h @ w_mlp_in))
    x = x + rms_norm(u @ w_mlp_out, post_mlp_g)
    return x


def setup_inputs(seed: int = 0) -> dict:
    key = jax.random.key(seed)
    ks = jax.random.split(key, 20)
    f32 = jnp.float32

    def nrm(k, shape, scale):
        return jax.random.normal(k, shape, f32) * scale

    def gain(k, shape):
        return 1.0 + 0.02 * jax.random.normal(k, shape, f32)

    return {
        "x": jax.random.normal(ks[0], (BATCH, SEQ, D_MODEL), f32),
        "meta_tokens": nrm(ks[1], (N_META, D_MODEL), 1.0),
        "pre_mix_g": gain(ks[2], (DEPTH, D_MODEL)),
        "q_norm_g": gain(ks[3], (DEPTH, HEAD_DIM)),
        "k_norm_g": gain(ks[4], (DEPTH, HEAD_DIM)),
        "w_in": nrm(ks[5], (DEPTH, D_MODEL, IN_WIDTH), D_MODEL ** -0.5),
        "w_attn_br": nrm(ks[6], (DEPTH, ATTN_WIDTH, D_MODEL), ATTN_WIDTH ** -0.5),
        "w_pool_grp": nrm(ks[7], (DEPTH, N_POOL_GROUPS, POOL_GROUP_DIM, POOL_GROUP_DIM), POOL_GROUP_DIM ** -0.5),
        "pool_scale": gain(ks[8], (DEPTH, POOL_WIDTH)),
        "w_pool_br": nrm(ks[9], (DEPTH, POOL_WIDTH, D_MODEL), POOL_WIDTH ** -0.5),
        "w_out": nrm(ks[10], (DEPTH, D_MODEL, D_MODEL), D_MODEL ** -0.5),
        "post_mix_g": gain(ks[11], (DEPTH, D_MODEL)),
        "pre_mlp_g": gain(ks[12], (DEPTH, D_MODEL)),
        "w_mlp_in": nrm(ks[13], (DEPTH, D_MODEL, D_FF), D_MODEL ** -0.5),
        "w_mlp_out": nrm(ks[14], (DEPTH, D_FF, D_MODEL), D_FF ** -0.5),
        "post_mlp_g": gain(ks[15], (DEPTH, D_MODEL)),
    }


def reference(x, meta_tokens, pre_mix_g, q_norm_g, k_norm_g, w_in, w_attn_br,
              w_pool_grp, pool_scale, w_pool_br, w_out, post_mix_g,
              pre_mlp_g, w_mlp_in, w_mlp_out, post_mlp_g):
    b, n_real, d = x.shape
    rows = n_real // GRID_W
    meta = jnp.broadcast_to(meta_tokens.astype(x.dtype)[None], (b, N_META, d))
    h = jnp.concatenate([meta, x], axis=1)
    rope = axial_rope_tables(rows)
    for i in range(DEPTH):
        h = hybrid_layer(h, rope, pre_mix_g[i], q_norm_g[i], k_norm_g[i], w_in[i],
                         w_attn_br[i], w_pool_grp[i], pool_scale[i], w_pool_br[i],
                         w_out[i], post_mix_g[i], pre_mlp_g[i], w_mlp_in[i],
                         w_mlp_out[i], post_mlp_g[i])
    return h[:, N_META:]
```

```python
import math
from contextlib import ExitStack

import numpy as np
import ml_dtypes
import concourse.bass as bass
import concourse.mybir as mybir
from concourse.bass_utils import run_bass_kernel_spmd

F32 = mybir.dt.float32
BF16 = mybir.dt.bfloat16
AF = mybir.ActivationFunctionType
ALU = mybir.AluOpType
AX = mybir.AxisListType

D = 1024
SEQ = 8192
BATCH = 4
N_META = 16
LTOT = SEQ + N_META
NQ = 4096
NG = 8
GT = 512
NKT = 65
EPS = 1e-6
NSLAB = 30
NSLOT = 4

ENGS = ("pe", "act", "dve", "pool", "sp")


class Buf:
    __slots__ = ("name", "last_w", "readers")

    def __init__(self, name):
        self.name = name
        self.last_w = None
        self.readers = {}


class Ins:
    __slots__ = ("eng", "fn", "deps", "needs_inc", "ticket", "dma_sem", "dma_val", "epoch")

    def __init__(self, eng, fn, epoch):
        self.eng = eng
        self.fn = fn
        self.deps = []
        self.needs_inc = False
        self.ticket = None
        self.dma_sem = None
        self.dma_val = None
        self.epoch = epoch


class Sched:
    def __init__(self):
        self.streams = {e: [] for e in ENGS}
        self.dma_counts = {}
        self.epoch = 0
        self.n_epochs = 1

    def next_epoch(self):
        self.epoch += 1
        self.n_epochs = self.epoch + 1

    def _collect(self, reads, writes):
        deps = []
        for b in reads:
            if b.last_w is not None:
                deps.append(b.last_w)
        for b in writes:
            if b.last_w is not None:
                deps.append(b.last_w)
            deps.extend(b.readers.values())
        return [("d", d[1], self.dma_counts[d[1]]) if d[0] == "d" else d for d in deps]

    def _finish(self, entry, key, reads, writes):
        for b in reads:
            b.readers[key] = entry
        for b in writes:
            b.last_w = entry
            b.readers = {}

    def op(self, eng, fn, reads=(), writes=()):
        ins = Ins(eng, fn, self.epoch)
        for d in self._collect(reads, writes):
            if d[0] == "c" and d[1].eng == eng and eng == "pe":
                continue
            ins.deps.append(d)
        self.streams[eng].append(ins)
        self._finish(("c", ins), eng, reads, writes)
        return ins

    def dma(self, queue, fn, semkey, reads=(), writes=()):
        ins = Ins(queue, fn, self.epoch)
        ins.deps = list(self._collect(reads, writes))
        c = self.dma_counts.get(semkey, 0) + 16
        self.dma_counts[semkey] = c
        ins.dma_sem = semkey
        ins.dma_val = c
        self.streams[queue].append(ins)
        self._finish(("d", semkey, c), "dma:" + semkey, reads, writes)
        return ins

    def wait_all(self, eng, bufs):
        ins = Ins(eng, None, self.epoch)
        ins.deps = list(self._collect((), bufs))
        self.streams[eng].append(ins)
        return ins

    def dma_keys(self):
        return list(self.dma_counts.keys())

    def emit(self, block, eng_sems, dma_sems):
        for e in ENGS:
            for ins in self.streams[e]:
                for d in ins.deps:
                    if d[0] == "c":
                        d[1].needs_inc = True
        for e in ENGS:
            t = {}
            for ins in self.streams[e]:
                if ins.needs_inc:
                    t[ins.epoch] = t.get(ins.epoch, 0) + 1
                    ins.ticket = t[ins.epoch]
        stats = {}

        def run(e, engobj):
            waited = {}
            nw = 0
            for ins in self.streams[e]:
                need = {}
                for d in ins.deps:
                    if d[0] == "c":
                        k, v = ("e", d[1].eng, d[1].epoch), d[1].ticket
                    else:
                        k, v = ("d", d[1]), d[2]
                    if waited.get(k, 0) >= v:
                        continue
                    if need.get(k, 0) < v:
                        need[k] = v
                for k, v in need.items():
                    sem = eng_sems[(k[1], k[2])] if k[0] == "e" else dma_sems[k[1]]
                    engobj.wait_ge(sem, v)
                    waited[k] = v
                    nw += 1
                if ins.fn is None:
                    continue
                bi = ins.fn(engobj)
                if ins.dma_sem is not None:
                    bi.then_inc(dma_sems[ins.dma_sem], 16)
                elif ins.needs_inc:
                    bi.then_inc(eng_sems[(e, ins.epoch)], 1)
            stats[e] = (len(self.streams[e]), nw)

        @block.tensor
        def _(eng):
            run("pe", eng)

        @block.scalar
        def _(eng):
            run("act", eng)

        @block.vector
        def _(eng):
            run("dve", eng)

        @block.gpsimd
        def _(eng):
            run("pool", eng)

        @block.sync
        def _(eng):
            run("sp", eng)

        return stats


def _mm(out, lhsT, rhs, start, stop):
    return lambda e: e.matmul(out, lhsT, rhs, start=start, stop=stop)


def _tr(out, in_, idn):
    return lambda e: e.transpose(out, in_, idn)


def _act(out, in_, func, **kw):
    return lambda e: e.activation(out=out, in_=in_, func=func, **kw)


def _tt(out, in0, in1, op):
    return lambda e: e.tensor_tensor(out=out, in0=in0, in1=in1, op=op)


def _ts(out, in0, s1, op0, s2=None, op1=None):
    if op1 is None:
        return lambda e: e.tensor_scalar(out=out, in0=in0, scalar1=s1, scalar2=None, op0=op0)
    return lambda e: e.tensor_scalar(out=out, in0=in0, scalar1=s1, scalar2=s2, op0=op0, op1=op1)


def _stt(out, in0, scalar, in1, op0, op1):
    return lambda e: e.scalar_tensor_tensor(out=out, in0=in0, scalar=scalar, in1=in1, op0=op0, op1=op1)


def _recip(out, in_):
    return lambda e: e.reciprocal(out=out, in_=in_)


def _red(out, in_):
    return lambda e: e.tensor_reduce(out=out, in_=in_, axis=AX.X, op=ALU.add)


def _dma(out, in_):
    return lambda e: e.dma_start(out=out, in_=in_)


def _copy(out, in_):
    return lambda e: e.tensor_copy(out=out, in_=in_)


def _memset(ap, v):
    return lambda e: e.memset(ap, v)


def build_program(dbg=False, n_groups=NG, n_kt=NKT):
    nc = bass.Bass("TRN2", target_bir_lowering=False)

    def din(name, shape, dt=F32):
        return nc.dram_tensor(name, list(shape), dt, kind="ExternalInput").ap()

    xkv = din("xkv", [NKT * 128, D])
    xq = din("xq", [NQ + 16, D])
    w_in = din("w_in", [D, 4096])
    w_attn = din("w_attn_br", [D, D])
    w_grp = din("w_pool_grp", [4, 128, 128])
    w_pbr = din("w_pool_br", [512, D])
    w_out = din("w_out", [D, D])
    w_mi = din("w_mlp_in", [D, 4096])
    w_mo = din("w_mlp_out", [4096, D])
    g_premix_c = din("g_premix_c", [128, 8])
    g_premlp_c = din("g_premlp_c", [128, 8])
    g_pscale_c = din("g_pscale_c", [128, 4])
    g_postmix = din("g_postmix", [D])
    g_postmlp = din("g_postmlp", [D])
    g_q = din("g_q", [128])
    g_qs = din("g_qs", [128])
    g_k = din("g_k", [128])
    g_ks = din("g_ks", [128])
    ropeK = din("ropeK", [NKT, 128, 256])
    ropeQ = din("ropeQ", [NQ // 128, 128, 256])
    invtail = din("invtail", [NG * 4 * 16])
    ident_d = din("ident", [128, 128], BF16)
    out = nc.dram_tensor("out", [NQ, D], F32, kind="ExternalOutput").ap()
    wsl = nc.dram_tensor("wsl", [NSLAB, 128, 4096], BF16, kind="Internal").ap()
    wgrp_d = nc.dram_tensor("wgrp_b", [128, 4, 128], BF16, kind="Internal").ap()
    if dbg:
        dbg_kt = nc.dram_tensor("dbg_kt", [128, 2, LTOT], BF16, kind="ExternalOutput").ap()
        dbg_v = nc.dram_tensor("dbg_v", [128, NKT, 256], BF16, kind="ExternalOutput").ap()
        dbg_R = {k: nc.dram_tensor("dbg_" + k, [128, 32, 512], BF16, kind="ExternalOutput").ap()
                 for k in ("b3", "b4", "b5", "b6")}
        dbg_hT = nc.dram_tensor("dbg_hT", [128, 8, 528], BF16, kind="ExternalOutput").ap()
        dbg_x1 = nc.dram_tensor("dbg_x1", [128, 4, D], F32, kind="ExternalOutput").ap()

    def dump_R(key):
        if dbg:
            S.dma("sp", _dma(dbg_R[key], R[:]), "dbg", reads=B_R, writes=[B_dbg])

    S = Sched()
    with ExitStack() as st:
        def sb(name, shape, dt):
            return st.enter_context(nc.sbuf_tensor(name, list(shape), dt))

        KT = sb("KT", [128, 2, LTOT], BF16)
        V = sb("V", [128, NKT, 256], BF16)
        R = sb("R", [128, 32, 512], BF16)
        xres = sb("xres", [128, 4, D], F32)
        xhalo = sb("xhalo", [16, D], F32)
        hT = sb("hT", [128, 8, 528], BF16)
        TB = [sb(f"TB{i}", [128, D], F32) for i in range(5)]
        hbf = [sb(f"hbf{i}", [128, D], BF16) for i in range(2)]
        qrot = [sb(f"qrot{i}", [128, D], BF16) for i in range(2)]
        pooled = [sb(f"pooled{i}", [128, 512], BF16) for i in range(4)]
        gpostmix = sb("gpostmix", [128, D], F32)
        gpostmlp = sb("gpostmlp", [128, D], F32)
        gcolmix = sb("gcolmix", [128, 8], F32)
        gcolmlp = sb("gcolmlp", [128, 8], F32)
        pscale = sb("pscale", [128, 4], F32)
        gq = sb("gq", [128, 128], F32)
        gqs = sb("gqs", [128, 128], F32)
        gk = sb("gk", [128, 128], F32)
        gks = sb("gks", [128, 128], F32)
        invt = sb("invt", [128, 64], F32)
        ident = sb("ident_sb", [128, 128], BF16)
        ones = sb("ones", [128, 128], BF16)
        sel32 = sb("sel32", [128, 128], F32)
        epsT = sb("epsT", [128, 1], F32)
        gmx = sb("gmx", [128, 2], F32)
        negc = sb("negc", [128, 1], F32)
        rtab = [sb(f"rtab{i}", [128, 256], F32) for i in range(2)]
        cs = [sb(f"cs{i}", [128, 256], F32) for i in range(2)]
        ss = [sb(f"ss{i}", [128, 8], F32) for i in range(4)]
        sd = [sb(f"sd{i}", [128, 8], F32) for i in range(4)]
        rstd = [sb(f"rstd{i}", [128, 8], F32) for i in range(4)]
        ssh = [sb(f"ssh{i}", [128, 8], F32) for i in range(4)]
        sdh = [sb(f"sdh{i}", [128, 8], F32) for i in range(4)]
        rstdh = [sb(f"rstdh{i}", [128, 8], F32) for i in range(4)]
        tail16 = sb("tail16", [128, 16], F32)
        wgrp = sb("wgrp", [128, 4, 128], BF16)
        ring = [sb(f"ring{i}", [128, 4096], BF16) for i in range(NSLOT)]
        PS = [st.enter_context(nc.psum_tensor(f"ps{i}", [128, 1024], F32)) for i in range(4)]

        def bank(b):
            return PS[b // 2][:, (b % 2) * 512:(b % 2) * 512 + 512]

        def bankb(b):
            return bank(b).bitcast(BF16)

        B_KT = [Buf(f"KT{i}") for i in range(NKT)]
        B_V = [Buf(f"V{i}") for i in range(NKT)]
        B_R = [Buf(f"R{i}") for i in range(32)]
        B_x = [Buf(f"x{i}") for i in range(4)]
        B_xh = Buf("xhalo")
        B_hT = [Buf(f"hT{i}") for i in range(4)]
        B_hTh = Buf("hTh")
        B_TB = [Buf(f"TB{i}") for i in range(5)]
        B_hbf = [Buf(f"hbf{i}") for i in range(2)]
        B_qrot = [Buf(f"qrot{i}") for i in range(2)]
        B_pooled = [Buf(f"pooled{i}") for i in range(4)]
        B_const = Buf("const")
        B_negc = Buf("negc")
        B_rtab = [Buf(f"rtab{i}") for i in range(2)]
        B_cs = [Buf(f"cs{i}") for i in range(2)]
        B_ss = [Buf(f"ss{i}") for i in range(4)]
        B_sd = [Buf(f"sd{i}") for i in range(4)]
        B_rstd = [Buf(f"rstd{i}") for i in range(4)]
        B_ssh = [Buf(f"ssh{i}") for i in range(4)]
        B_sdh = [Buf(f"sdh{i}") for i in range(4)]
        B_rstdh = [Buf(f"rstdh{i}") for i in range(4)]
        B_tail = Buf("tail16")
        B_invt = Buf("invt")
        B_wgrp = Buf("wgrp")
        B_wgrpd = Buf("wgrpd")
        B_ring = [Buf(f"ring{i}") for i in range(NSLOT)]
        B_ws = [Buf(f"ws{i}") for i in range(NSLAB)]
        B_bk = [Buf(f"bank{i}") for i in range(8)]
        B_out = [Buf(f"out{i}") for i in range(4)]
        B_dbg = Buf("dbg")

        B_cl = []

        def cload(dst, src):
            b_ = Buf("c%d" % len(B_cl))
            B_cl.append(b_)
            S.dma("sp", _dma(dst, src), "const", writes=[b_])

        cload(ident[:], ident_d)
        cload(gcolmix[:], g_premix_c)
        cload(gcolmlp[:], g_premlp_c)
        cload(pscale[:], g_pscale_c)
        cload(gk[:], g_k.partition_broadcast(128))
        cload(gks[:], g_ks.partition_broadcast(128))
        cload(gq[:], g_q.partition_broadcast(128))
        cload(gqs[:], g_qs.partition_broadcast(128))
        cload(gpostmix[:], g_postmix.partition_broadcast(128))
        cload(gpostmlp[:], g_postmlp.partition_broadcast(128))
        S.op("dve", _memset(ones[:], 1.0), writes=[B_const])
        S.op("dve", _memset(sel32[:], 1.0), writes=[B_const])
        S.op("dve", _memset(epsT[:], EPS), reads=B_cl, writes=[B_const])
        S.op("dve", lambda e: e.tensor_reduce(out=gmx[:, 0:1], in_=gq[:, :], axis=AX.X, op=ALU.max, apply_absolute_value=True),
             reads=[B_const], writes=[B_negc])
        S.op("dve", lambda e: e.tensor_reduce(out=gmx[:, 1:2], in_=gk[:, :], axis=AX.X, op=ALU.max, apply_absolute_value=True),
             reads=[B_const, B_negc], writes=[B_negc])
        S.op("dve", _tt(negc[:, :], gmx[:, 0:1], gmx[:, 1:2], ALU.mult), reads=[B_negc], writes=[B_negc])
        S.op("dve", _ts(negc[:, :], negc[:, :], -math.sqrt(128.0), ALU.mult), reads=[B_negc], writes=[B_negc])

        def cast(slab, pieces, key):
            for (src, c0, nch, ncols) in pieces:
                dst = wsl[slab][:, c0:c0 + nch * ncols].rearrange("p (c n) -> p c n", c=nch)
                srcv = src.rearrange("(c p) n -> p c n", p=128)
                S.dma("pool", _dma(dst, srcv), key, writes=[B_ws[slab]])

        cast(0, [(w_in[:, 1024:1536], 0, 8, 512)], "cast0")
        def early_casts():
            cast(1, [(w_in[:, 0:512], 0, 8, 512)], "castA")
            cast(2, [(w_in[:, 512:1024], 0, 8, 512)], "castA")
            cast(3, [(w_in[:, 1536:2048], 0, 8, 512)], "castA")
            S.dma("pool", _dma(wgrp_d, w_grp.rearrange("g c d -> c g d")), "castA", writes=[B_wgrpd])

        def deferred_casts():
            for j in range(8):
                cast(4 + j, [
                    (w_attn[:, j * 128:(j + 1) * 128], 0, 8, 128),
                    (w_in[:, 2048 + j * 128:2048 + (j + 1) * 128], 1024, 8, 128),
                    (w_in[:, 3072 + j * 128:3072 + (j + 1) * 128], 2048, 8, 128),
                    (w_pbr[:, j * 128:(j + 1) * 128], 3072, 4, 128),
                ], "castC")
            for hh in range(2):
                cast(12 + hh, [(w_out[:, hh * 512:(hh + 1) * 512], 0, 8, 512)], "castC")
            for s_ in range(8):
                cast(14 + s_, [(w_mi[:, s_ * 512:(s_ + 1) * 512], 0, 8, 512)], "castB")
            for hh in range(2):
                for blk in range(4):
                    cast(22 + hh * 4 + blk,
                         [(w_mo[blk * 1024:(blk + 1) * 1024, hh * 512:(hh + 1) * 512], 0, 8, 512)], "castB")

        ring_ctr = [0]

        def ring_load(slab, ncols=4096):
            slot = ring_ctr[0] % NSLOT
            ring_ctr[0] += 1
            S.dma("sp", _dma(ring[slot][:, 0:ncols], wsl[slab][:, 0:ncols]), f"ring{slot}",
                  reads=[B_ws[slab]], writes=[B_ring[slot]])
            return slot

        def slab3(slot):
            return ring[slot][:].rearrange("p (c n) -> p c n", c=8)

        class Ctx:
            pass

        def mk_ctx(i, tA, bA, tU, bU, rt, brt, cs_, bcs, rot, brot, hb, bhb):
            c = Ctx()
            c.ss, c.b_ss = ss[i], B_ss[i]
            c.sd, c.b_sd = sd[i], B_sd[i]
            c.rstd, c.b_rstd = rstd[i], B_rstd[i]
            c.ssh, c.b_ssh = ssh[i], B_ssh[i]
            c.sdh, c.b_sdh = sdh[i], B_sdh[i]
            c.rstdh, c.b_rstdh = rstdh[i], B_rstdh[i]
            c.tA, c.b_tA, c.tU, c.b_tU = tA, bA, tU, bU
            c.rtab, c.b_rtab, c.cs, c.b_cs = rt, brt, cs_, bcs
            c.rot, c.b_rot, c.hbf, c.b_hbf = rot, brot, hb, bhb
            return c

        ctxB = [mk_ctx(i, TB[i], [B_TB[i]], TB[2 + i], [B_TB[2 + i]], rtab[i], [B_rtab[i]], cs[i], [B_cs[i]],
                       qrot[i], [B_qrot[i]], hbf[i], [B_hbf[i]]) for i in range(2)]
        ctxA = []
        for i in range(4):
            ctxA.append(mk_ctx(
                i,
                R[:, i, :].bitcast(F32), [B_R[i]],
                R[:, 4 + i, :].bitcast(F32), [B_R[4 + i]],
                R[:, 12 + i, :].bitcast(F32), [B_R[12 + i]],
                R[:, 16 + i, :].bitcast(F32), [B_R[16 + i]],
                R[:, 8 + i, :], [B_R[8 + i]],
                R[:, 20 + 2 * i:22 + 2 * i, :].rearrange("p a n -> p (a n)"), [B_R[20 + 2 * i], B_R[21 + 2 * i]]))

        def emit_h(xap, xbuf, rows, cx, bT, gcol, dsts):
            S.op("act", _act(cx.hbf[0:rows, :], xap, AF.Square, accum_out=cx.ss[0:rows, 0:1]),
                 reads=[xbuf], writes=[cx.b_ss] + cx.b_hbf)
            S.op("act", _act(cx.sd[0:rows, 0:1], cx.ss[0:rows, 0:1], AF.Sqrt, bias=epsT[0:rows, :], scale=1.0 / D),
                 reads=[cx.b_ss, B_const], writes=[cx.b_sd])
            S.op("dve", _recip(cx.rstd[0:rows, 0:1], cx.sd[0:rows, 0:1]), reads=[cx.b_sd], writes=[cx.b_rstd])
            S.op("act", _act(cx.hbf[0:rows, :], xap, AF.Copy, scale=cx.rstd[0:rows, 0:1]),
                 reads=[xbuf, cx.b_rstd], writes=cx.b_hbf)
            pb = bankb(bT)
            for c in range(8):
                S.op("pe", _tr(pb[:, c * 128:c * 128 + rows], cx.hbf[0:rows, c * 128:(c + 1) * 128], ident[0:rows, 0:rows]),
                     reads=cx.b_hbf + [B_const], writes=[B_bk[bT]])
            pb3 = pb.rearrange("p (c n) -> p c n", c=8)
            for (hc0, pc0, ncol, dbuf) in dsts:
                S.op("dve", _tt(hT[:, :, hc0:hc0 + ncol], pb3[:, :, pc0:pc0 + ncol],
                                gcol[:].unsqueeze(2).broadcast_to([128, 8, ncol]), ALU.mult),
                     reads=[B_bk[bT], B_const], writes=[dbuf])

        def nr_square(X, xbank_bufs, H, cx):
            n = H * 128
            S.op("act", _act(cx.tA[:, 0:n], X, AF.Square), reads=xbank_bufs, writes=cx.b_tA)

        def nr_rest(X, xbank_bufs, H, cx, gt, gst):
            n = H * 128
            tA, tU = cx.tA, cx.tU
            S.op("dve", _red(cx.ssh[:, 0:H], tA[:, 0:n].rearrange("p (h d) -> p h d", h=H)),
                 reads=cx.b_tA, writes=[cx.b_ssh])
            S.op("act", _act(cx.sdh[:, 0:H], cx.ssh[:, 0:H], AF.Sqrt, bias=epsT[:, :], scale=1.0 / 128),
                 reads=[cx.b_ssh, B_const], writes=[cx.b_sdh])
            S.op("dve", _recip(cx.rstdh[:, 0:H], cx.sdh[:, 0:H]), reads=[cx.b_sdh], writes=[cx.b_rstdh])
            S.op("pool", _tt(cx.cs[:, 0:128], cx.rtab[:, 0:128], gt[:], ALU.mult),
                 reads=cx.b_rtab + [B_const], writes=cx.b_cs)
            S.op("pool", _tt(cx.cs[:, 128:256], cx.rtab[:, 128:256], gst[:], ALU.mult),
                 reads=cx.b_rtab + [B_const], writes=cx.b_cs)
            X5 = X.rearrange("p (h a b j) -> p h a b j", h=H, a=2, b=2, j=32)
            T5 = tA[:, 0:n].rearrange("p (h a b j) -> p h a b j", h=H, a=2, b=2, j=32)
            Sg = cx.cs[:, 128:256].rearrange("p (a b j) -> p a b j", a=2, b=2, j=32)
            for bsel in range(2):
                S.op("dve", _tt(T5[:, :, :, bsel, :], X5[:, :, :, 1 - bsel, :],
                                Sg[:, :, bsel, :].unsqueeze(1).broadcast_to([128, H, 2, 32]), ALU.mult),
                     reads=xbank_bufs + cx.b_cs, writes=cx.b_tA)
            S.op("dve", _tt(tU[:, 0:n].rearrange("p (h d) -> p h d", h=H), X.rearrange("p (h d) -> p h d", h=H),
                            cx.cs[:, 0:128].unsqueeze(1).broadcast_to([128, H, 128]), ALU.mult),
                 reads=xbank_bufs + cx.b_cs, writes=cx.b_tU)
            S.op("pool", _tt(tU[:, 0:n], tU[:, 0:n], tA[:, 0:n], ALU.add),
                 reads=cx.b_tA + cx.b_tU, writes=cx.b_tU)
            S.op("dve", _tt(cx.rot[:, 0:n].rearrange("p (h d) -> p h d", h=H), tU[:, 0:n].rearrange("p (h d) -> p h d", h=H),
                            cx.rstdh[:, 0:H].unsqueeze(2).broadcast_to([128, H, 128]), ALU.mult),
                 reads=cx.b_tU + [cx.b_rstdh], writes=cx.b_rot)

        kvslot = ring_load(0)
        wkv = slab3(kvslot)

        def a_ctx(kt):
            return ctxA[kt % 4]

        def sqbuf(kt):
            return R[:, 28 + kt % 4, :].bitcast(F32), [B_R[28 + kt % 4]]

        def S1(kt):
            q = kt % 4
            cx = a_ctx(kt)
            xap, xbuf = xres[:, q, :], B_x[q]
            S.dma("sp", _dma(xap, xkv[kt * 128:(kt + 1) * 128, :]), f"xa{q}", writes=[xbuf])
            S.dma("sp", _dma(cx.rtab, ropeK[kt]), f"rt{q}", writes=cx.b_rtab)
            S.op("act", _act(cx.hbf[:, :], xap, AF.Square, accum_out=cx.ss[:, 0:1]), reads=[xbuf],
                 writes=[cx.b_ss] + cx.b_hbf)
            S.op("act", _act(cx.sd[:, 0:1], cx.ss[:, 0:1], AF.Sqrt, bias=epsT[:, :], scale=1.0 / D),
                 reads=[cx.b_ss, B_const], writes=[cx.b_sd])
            S.op("dve", _recip(cx.rstd[:, 0:1], cx.sd[:, 0:1]), reads=[cx.b_sd], writes=[cx.b_rstd])

        def S1b(kt):
            q = kt % 4
            cx = a_ctx(kt)
            xap, xbuf = xres[:, q, :], B_x[q]
            S.op("act", _act(cx.hbf[:, :], xap, AF.Copy, scale=cx.rstd[:, 0:1]),
                 reads=[xbuf, cx.b_rstd], writes=cx.b_hbf)

        def S2(kt):
            q = kt % 4
            cx = a_ctx(kt)
            bT, bkv = kt % 3, 3 + kt % 3
            pb = bankb(bT)
            for c in range(8):
                S.op("pe", _tr(pb[:, c * 128:(c + 1) * 128], cx.hbf[:, c * 128:(c + 1) * 128], ident[:, :]),
                     reads=cx.b_hbf + [B_const], writes=[B_bk[bT]])
            pb3 = pb.rearrange("p (c n) -> p c n", c=8)
            S.op("dve", _tt(hT[:, :, 8 + q * 128:8 + (q + 1) * 128], pb3[:, :, :],
                            gcolmix[:].unsqueeze(2).broadcast_to([128, 8, 128]), ALU.mult),
                 reads=[B_bk[bT], B_const], writes=[B_hT[q]])
            for c in range(8):
                S.op("pe", _mm(bank(bkv), hT[:, c, 8 + q * 128:8 + (q + 1) * 128], wkv[:, c, :], c == 0, c == 7),
                     reads=[B_hT[q], B_ring[kvslot]], writes=[B_bk[bkv]])

        def S3(kt):
            cx = a_ctx(kt)
            bkv = 3 + kt % 3
            X = bank(bkv)[:, 0:256]
            xb_ = [B_bk[bkv]]
            sq, bsq = sqbuf(kt)
            S.op("act", _act(V[:, kt, :], bank(bkv)[:, 256:512], AF.Copy), reads=xb_, writes=[B_V[kt]])
            S.op("act", _act(sq[:, :], X, AF.Square), reads=xb_, writes=bsq)
            S.op("pool", _tt(cx.cs[:, 0:128], cx.rtab[:, 0:128], gk[:], ALU.mult),
                 reads=cx.b_rtab + [B_const], writes=cx.b_cs)
            S.op("pool", _tt(cx.cs[:, 128:256], cx.rtab[:, 128:256], gks[:], ALU.mult),
                 reads=cx.b_rtab + [B_const], writes=cx.b_cs)
            S.op("dve", _red(cx.ssh[:, 0:2], sq[:, :].rearrange("p (h d) -> p h d", h=2)),
                 reads=bsq, writes=[cx.b_ssh])
            X5 = X.rearrange("p (h a b j) -> p h a b j", h=2, a=2, b=2, j=32)
            T5 = cx.tA[:, :].rearrange("p (h a b j) -> p h a b j", h=2, a=2, b=2, j=32)
            Sg = cx.cs[:, 128:256].rearrange("p (a b j) -> p a b j", a=2, b=2, j=32)
            for bsel in range(2):
                S.op("dve", _tt(T5[:, :, :, bsel, :], X5[:, :, :, 1 - bsel, :],
                                Sg[:, :, bsel, :].unsqueeze(1).broadcast_to([128, 2, 2, 32]), ALU.mult),
                     reads=xb_ + cx.b_cs, writes=cx.b_tA)
            S.op("dve", _tt(cx.tU[:, :].rearrange("p (h d) -> p h d", h=2), X.rearrange("p (h d) -> p h d", h=2),
                            cx.cs[:, 0:128].unsqueeze(1).broadcast_to([128, 2, 128]), ALU.mult),
                 reads=xb_ + cx.b_cs, writes=cx.b_tU)

        def S4(kt):
            cx = a_ctx(kt)
            S.op("act", _act(cx.sdh[:, 0:2], cx.ssh[:, 0:2], AF.Sqrt, bias=epsT[:, :], scale=1.0 / 128),
                 reads=[cx.b_ssh, B_const], writes=[cx.b_sdh])
            S.op("dve", _recip(cx.rstdh[:, 0:2], cx.sdh[:, 0:2]), reads=[cx.b_sdh], writes=[cx.b_rstdh])
            S.op("pool", _tt(cx.tU[:, :], cx.tU[:, :], cx.tA[:, :], ALU.add),
                 reads=cx.b_tA + cx.b_tU, writes=cx.b_tU)

        def S4b(kt):
            cx = a_ctx(kt)
            S.op("dve", _tt(cx.rot[:, 0:256].rearrange("p (h d) -> p h d", h=2), cx.tU[:, :].rearrange("p (h d) -> p h d", h=2),
                            cx.rstdh[:, 0:2].unsqueeze(2).broadcast_to([128, 2, 128]), ALU.mult),
                 reads=cx.b_tU + [cx.b_rstdh], writes=cx.b_rot)

        def S5(kt):
            cx = a_ctx(kt)
            bkt = 6 + kt % 2
            pb = bankb(bkt)
            for h in range(2):
                S.op("pe", _tr(pb[:, h * 128:(h + 1) * 128], cx.rot[:, h * 128:(h + 1) * 128], ident[:]),
                     reads=cx.b_rot + [B_const], writes=[B_bk[bkt]])
            ntok = 16 if kt == 0 else 128
            col0 = 0 if kt == 0 else 16 + (kt - 1) * 128
            S.op("dve", _copy(KT[:, :, col0:col0 + ntok], pb.rearrange("p (c n) -> p c n", c=8)[:, 0:2, 0:ntok]),
                 reads=[B_bk[bkt]], writes=[B_KT[kt]])

        stages = [S1, S1b, S2, S3, S4, S4b, S5]
        for step in range(n_kt + len(stages) - 1):
            if step == min(8, n_kt - 1):
                early_casts()
            for si, fn in enumerate(stages):
                k_ = step - si
                if 0 <= k_ < n_kt:
                    fn(k_)

        if dbg:
            S.dma("sp", _dma(dbg_kt, KT[:]), "dbg", reads=B_KT, writes=[B_dbg])
            S.dma("sp", _dma(dbg_v, V[:]), "dbg", reads=B_V, writes=[B_dbg])

        scale_qk = 1.0 / math.sqrt(128.0)

        def modulo(stage_fns, n):
            for step in range(n + len(stage_fns) - 1):
                for si in reversed(range(len(stage_fns))):
                    t_ = step - si
                    if 0 <= t_ < n:
                        stage_fns[si](t_)

        for G in range(n_groups):
            S.next_epoch()
            r0 = G * GT
            sq0 = ring_load(1)
            sq1 = ring_load(2)
            wq = [slab3(sq0), slab3(sq1)]
            wqb = [B_ring[sq0], B_ring[sq1]]
            S.dma("sp", _dma(xhalo[0:8, :], xq[r0:r0 + 8, :]), "xlh", writes=[B_xh])
            S.dma("sp", _dma(xhalo[8:16, :], xq[8 + r0 + GT:8 + r0 + GT + 8, :]), "xlh", writes=[B_xh])
            for t in range(4):
                S.dma("sp", _dma(xres[:, t, :], xq[8 + r0 + t * 128:8 + r0 + (t + 1) * 128, :]), f"xl{t}",
                      writes=[B_x[t]])
            S.dma("sp", _dma(invt[:], invtail[G * 64:(G + 1) * 64].partition_broadcast(128)), "invt", writes=[B_invt])

            def Ha(t, gcol_unused=None):
                cx = ctxB[t % 2]
                xap, xbuf = xres[:, t, :], B_x[t]
                S.op("act", _act(cx.hbf[:, :], xap, AF.Square, accum_out=cx.ss[:, 0:1]),
                     reads=[xbuf], writes=[cx.b_ss] + cx.b_hbf)
                S.op("act", _act(cx.sd[:, 0:1], cx.ss[:, 0:1], AF.Sqrt, bias=epsT[:, :], scale=1.0 / D),
                     reads=[cx.b_ss, B_const], writes=[cx.b_sd])
                S.op("dve", _recip(cx.rstd[:, 0:1], cx.sd[:, 0:1]), reads=[cx.b_sd], writes=[cx.b_rstd])

            def Ha2(t):
                cx = ctxB[t % 2]
                xap, xbuf = xres[:, t, :], B_x[t]
                S.op("act", _act(cx.hbf[:, :], xap, AF.Copy, scale=cx.rstd[:, 0:1]),
                     reads=[xbuf, cx.b_rstd], writes=cx.b_hbf)

            def mk_Hb(gcol):
                def Hb(t):
                    cx = ctxB[t % 2]
                    bT = 6 + t % 2
                    pb = bankb(bT)
                    for c in range(8):
                        S.op("pe", _tr(pb[:, c * 128:(c + 1) * 128], cx.hbf[:, c * 128:(c + 1) * 128], ident[:, :]),
                             reads=cx.b_hbf + [B_const], writes=[B_bk[bT]])
                    S.op("dve", _tt(hT[:, :, 8 + t * 128:8 + (t + 1) * 128], pb.rearrange("p (c n) -> p c n", c=8),
                                    gcol[:].unsqueeze(2).broadcast_to([128, 8, 128]), ALU.mult),
                         reads=[B_bk[bT], B_const], writes=[B_hT[t]])
                return Hb

            def Hh():
                emit_h(xhalo[:, :], B_xh, 16, ctxB[0], 6, gcolmix, [(0, 0, 8, B_hTh), (520, 8, 8, B_hTh)])

            def Qa(t):
                par = t % 2
                S.dma("sp", _dma(rtab[par][:], ropeQ[G * 4 + t]), f"rq{par}", writes=[B_rtab[par]])
                for hh in range(2):
                    for c in range(8):
                        S.op("pe", _mm(bank(2 * par + hh), hT[:, c, 8 + t * 128:8 + (t + 1) * 128], wq[hh][:, c, :], c == 0, c == 7),
                             reads=[B_hT[t], wqb[hh]], writes=[B_bk[2 * par + hh]])
                S.op("act", _act(TB[4][:, :], PS[par][:, :], AF.Square),
                     reads=[B_bk[2 * par], B_bk[2 * par + 1]], writes=[B_TB[4]])

            def Qb(t):
                par = t % 2
                cx = ctxB[par]
                X = PS[par][:, :]
                xb_ = [B_bk[2 * par], B_bk[2 * par + 1]]
                S.op("dve", _red(cx.ssh[:, 0:8], TB[4][:, :].rearrange("p (h d) -> p h d", h=8)),
                     reads=[B_TB[4]], writes=[cx.b_ssh])
                S.op("pool", _tt(cx.cs[:, 0:128], cx.rtab[:, 0:128], gq[:], ALU.mult),
                     reads=cx.b_rtab + [B_const], writes=cx.b_cs)
                S.op("pool", _tt(cx.cs[:, 128:256], cx.rtab[:, 128:256], gqs[:], ALU.mult),
                     reads=cx.b_rtab + [B_const], writes=cx.b_cs)
                X5 = X.rearrange("p (h a b j) -> p h a b j", h=8, a=2, b=2, j=32)
                T5 = cx.tA[:, :].rearrange("p (h a b j) -> p h a b j", h=8, a=2, b=2, j=32)
                Sg = cx.cs[:, 128:256].rearrange("p (a b j) -> p a b j", a=2, b=2, j=32)
                for bsel in range(2):
                    S.op("dve", _tt(T5[:, :, :, bsel, :], X5[:, :, :, 1 - bsel, :],
                                    Sg[:, :, bsel, :].unsqueeze(1).broadcast_to([128, 8, 2, 32]), ALU.mult),
                         reads=xb_ + cx.b_cs, writes=cx.b_tA)
                S.op("dve", _tt(cx.tU[:, :].rearrange("p (h d) -> p h d", h=8), X.rearrange("p (h d) -> p h d", h=8),
                                cx.cs[:, 0:128].unsqueeze(1).broadcast_to([128, 8, 128]), ALU.mult),
                     reads=xb_ + cx.b_cs, writes=cx.b_tU)

            def Qc(t):
                cx = ctxB[t % 2]
                S.op("act", _act(cx.sdh[:, 0:8], cx.ssh[:, 0:8], AF.Sqrt, bias=epsT[:, :], scale=1.0 / 128),
                     reads=[cx.b_ssh, B_const], writes=[cx.b_sdh])
                S.op("dve", _recip(cx.rstdh[:, 0:8], cx.sdh[:, 0:8]), reads=[cx.b_sdh], writes=[cx.b_rstdh])
                S.op("pool", _tt(cx.tU[:, :], cx.tU[:, :], cx.tA[:, :], ALU.add),
                     reads=cx.b_tA + cx.b_tU, writes=cx.b_tU)

            def Qc2(t):
                cx = ctxB[t % 2]
                S.op("pool", _tt(cx.rot[:, :].rearrange("p (h d) -> p h d", h=8), cx.tU[:, :].rearrange("p (h d) -> p h d", h=8),
                                cx.rstdh[:, 0:8].unsqueeze(2).broadcast_to([128, 8, 128]), ALU.mult),
                     reads=cx.b_tU + [cx.b_rstdh], writes=cx.b_rot)

            def Qd(t):
                par = t % 2
                cx = ctxB[par]
                bq = 4 + par
                pb = bankb(bq)
                for h in range(8):
                    S.op("pe", _tr(pb[:, h * 128:(h + 1) * 128], cx.rot[:, h * 128:(h + 1) * 128], ident[:]),
                         reads=cx.b_rot + [B_const], writes=[B_bk[bq]])
                S.op("act", _act(R[:, 0:8, t * 128:(t + 1) * 128], pb.rearrange("p (c n) -> p c n", c=8), AF.Copy),
                     reads=[B_bk[bq]], writes=B_R[0:8])

            Hh()
            modulo([Ha, Ha2, mk_Hb(gcolmix), Qa, Qb, Qc, Qc2, Qd], 4)
            if G == 0:
                dump_R("b3")
                if dbg:
                    S.dma("sp", _dma(dbg_hT, hT[:]), "dbg", reads=B_hT + [B_hTh], writes=[B_dbg])
            sp_ = ring_load(3)
            wp = slab3(sp_)
            for g in range(4):
                par = g % 2
                w_ = 2 << g
                bm, bh = par, 2 + par
                for c in range(8):
                    S.op("pe", _mm(bank(bm), wp[:, c, g * 128:(g + 1) * 128], hT[:, c, 8:520], c == 0, c == 7),
                         reads=B_hT + [B_ring[sp_]], writes=[B_bk[bm]])
                for c in range(8):
                    S.op("pe", _mm(bank(bh)[:, 0:8], wp[:, c, g * 128:(g + 1) * 128], hT[:, c, 0:8], c == 0, c == 7),
                         reads=[B_hTh, B_ring[sp_]], writes=[B_bk[bh]])
                for c in range(8):
                    S.op("pe", _mm(bank(bh)[:, 8:16], wp[:, c, g * 128:(g + 1) * 128], hT[:, c, 520:528], c == 0, c == 7),
                         reads=[B_hTh, B_ring[sp_]], writes=[B_bk[bh]])
                pbuf = TB[par]
                S.op("act", _act(pbuf[:, 8:520], bank(bm), AF.Copy), reads=[B_bk[bm]], writes=[B_TB[par]])
                S.op("act", _act(pbuf[:, 0:8], bank(bh)[:, 0:8], AF.Copy), reads=[B_bk[bh]], writes=[B_TB[par]])
                S.op("act", _act(pbuf[:, 520:528], bank(bh)[:, 8:16], AF.Copy), reads=[B_bk[bh]], writes=[B_TB[par]])
                s2 = TB[2 + par]
                s4 = TB[4]
                S.op("pool", _tt(s2[:, 1:528], pbuf[:, 0:527], pbuf[:, 1:528], ALU.add),
                     reads=[B_TB[par]], writes=[B_TB[2 + par]])
                ssum, bsum = s2, B_TB[2 + par]
                if g >= 1:
                    S.op("pool", _tt(s4[:, 2:527], s2[:, 1:526], s2[:, 3:528], ALU.add),
                         reads=[B_TB[2 + par]], writes=[B_TB[4]])
                    ssum, bsum = s4, B_TB[4]
                if g >= 2:
                    S.op("pool", _tt(s2[:, 4:525], s4[:, 2:523], s4[:, 6:527], ALU.add),
                         reads=[B_TB[4]], writes=[B_TB[2 + par]])
                    ssum, bsum = s2, B_TB[2 + par]
                if g >= 3:
                    S.op("pool", _tt(s4[:, 8:521], s2[:, 4:517], s2[:, 12:525], ALU.add),
                         reads=[B_TB[2 + par]], writes=[B_TB[4]])
                    ssum, bsum = s4, B_TB[4]
                S.op("dve", _stt(pooled[g][:, :], ssum[:, 8:520], 1.0 / w_, pbuf[:, 8:520], ALU.mult, ALU.subtract),
                     reads=[bsum, B_TB[par]], writes=[B_pooled[g]])
                io = g * 16
                S.op("dve", _tt(tail16[:, :], ssum[:, 504:520], invt[:, io:io + 16], ALU.mult),
                     reads=[bsum, B_invt], writes=[B_tail])
                S.op("dve", _tt(pooled[g][:, 496:512], tail16[:, :], pbuf[:, 504:520], ALU.subtract),
                     reads=[B_tail, B_TB[par]], writes=[B_pooled[g]])
            if G == 0:
                deferred_casts()
            NP = 6
            PE_EVERY = 4
            npairs = (n_kt - 1) // 2
            units = [[1 + 2 * j, 2 + 2 * j] for j in range(npairs)] + [[0]]
            ntiles = sum(len(u_) for u_ in units)
            pending_fin = [None]
            uctr = [0]

            def fin_a(h):
                aD, bD = TB[h % 2], B_TB[h % 2]
                S.op("dve", _copy(TB[2 + h % 2][:, 0:512], bank(6)), reads=[B_bk[6]], writes=[B_TB[2 + h % 2]])
                S.op("dve", _tt(TB[4][:, 0:512], aD[:, 0:512], aD[:, 512:1024], ALU.add), reads=[bD], writes=[B_TB[4]])

            def fin_b(h):
                bo, btot = 6, 7
                S.op("pe", _mm(bank(btot), sel32[:, :], TB[4][:, 0:512], False, True),
                     reads=[B_TB[4], B_const], writes=[B_bk[btot]])
                S.op("dve", _recip(TB[4][:, 512:1024], bank(btot)), reads=[B_bk[btot]], writes=[B_TB[4]])
                S.op("dve", _tt(R[:, 8 + h, :], TB[2 + h % 2][:, 0:512], TB[4][:, 512:1024], ALU.mult),
                     reads=[B_TB[2 + h % 2], B_TB[4]], writes=[B_R[8 + h]])

            for h in range(8):
                kv = h // 4
                bo, btot = 6, 7
                aD, bD = TB[h % 2], B_TB[h % 2]
                npe = [0]
                tc = [0]
                ninit = {"dve": 0, "pool": 0}

                def qk(uu, unit):
                    for idx, kt in enumerate(unit):
                        nk = 16 if kt == 0 else 128
                        col0 = 0 if kt == 0 else 16 + (kt - 1) * 128
                        S.op("pe", _mm(PS[uu % 3][0:nk, idx * 512:(idx + 1) * 512], KT[:, kv, col0:col0 + nk], R[:, h, :], True, True),
                             reads=[B_KT[kt], B_R[h]], writes=[B_bk[2 * (uu % 3) + idx]])

                def rest(uu, unit, pi):
                    nk = 16 if unit[0] == 0 else 128
                    w = 512 * len(unit)
                    g0 = 16 + 2 * (uu % NP)
                    ptb = R[0:nk, g0:g0 + 2, :].rearrange("p a n -> p (a n)")
                    pbufs = [B_R[g0], B_R[g0 + 1]][0:len(unit)]
                    sbufs = [B_bk[2 * (uu % 3) + i_] for i_ in range(len(unit))]
                    S.op("act", _act(ptb[:, 0:w], PS[uu % 3][0:nk, 0:w], AF.Exp, scale=scale_qk, bias=negc[0:nk, :]),
                         reads=sbufs + [B_negc], writes=pbufs)
                    for idx, kt in enumerate(unit):
                        first = (tc[0] + idx == 0)
                        last = (tc[0] + idx == ntiles - 1)
                        S.op("pe", _mm(bank(bo), V[0:nk, kt, kv * 128:(kv + 1) * 128], ptb[:, idx * 512:(idx + 1) * 512], first, last),
                             reads=[B_V[kt], pbufs[idx]], writes=[B_bk[bo]])
                    tc[0] += len(unit)
                    if unit[0] == 0:
                        S.op("dve", _tt(aD[0:nk, 0:512], aD[0:nk, 0:512], ptb[:, 0:512], ALU.add),
                             reads=pbufs + [bD], writes=[bD])
                        return
                    if pi % 2 == 1 and pi >= 3:
                        S.op("pe", _mm(bank(btot), ones[:, :], ptb[:, 512:1024], npe[0] == 0, False),
                             reads=[B_const, pbufs[1]], writes=[B_bk[btot]])
                        npe[0] += 1
                        S.op("dve", _tt(aD[:, 0:512], aD[:, 0:512], ptb[:, 0:512], ALU.add),
                             reads=[pbufs[0], bD], writes=[bD])
                        return
                    if ninit["dve"] == 0:
                        S.op("dve", _copy(aD[:, :], ptb[:, :]), reads=pbufs, writes=[bD])
                    else:
                        S.op("dve", _tt(aD[:, :], aD[:, :], ptb[:, :], ALU.add), reads=pbufs + [bD], writes=[bD])
                    ninit["dve"] += 1

                u0 = uctr[0]
                AH = 2
                for u in range(min(AH, len(units))):
                    qk(u0 + u, units[u])
                for u in range(len(units)):
                    if u + AH < len(units):
                        qk(u0 + u + AH, units[u + AH])
                    rest(u0 + u, units[u], u)
                    if u == 1 and pending_fin[0] is not None:
                        fin_b(pending_fin[0])
                        pending_fin[0] = None
                uctr[0] += len(units)
                fin_a(h)
                pending_fin[0] = h
            fin_b(pending_fin[0])
            if G == 0:
                S.dma("sp", _dma(wgrp[:], wgrp_d), "wgrp", reads=[B_wgrpd], writes=[B_wgrp])
            for g in range(4):
                bg = g % 3
                S.op("pe", _mm(bank(bg), wgrp[:, g, :], pooled[g][:, :], True, True),
                     reads=[B_wgrp, B_pooled[g]], writes=[B_bk[bg]])
                S.op("act", _act(R[:, 28 + g, :], bank(bg), AF.Copy, scale=pscale[:, g:g + 1]),
                     reads=[B_bk[bg], B_const], writes=[B_R[28 + g]])
            if G == 0:
                dump_R("b5")
            for j in range(8):
                sm = ring_load(4 + j, 3584)
                M = ring[sm]
                bm_ = B_ring[sm]
                base = 0 if j % 2 == 0 else 4
                bA, bG0, bG1, bPb = base, base + 1, base + 2, base + 3
                for c in range(8):
                    S.op("pe", _mm(bank(bA), M[:, c * 128:(c + 1) * 128], R[:, 8 + c, :], c == 0, c == 7),
                         reads=[bm_, B_R[8 + c]], writes=[B_bk[bA]])
                for c in range(8):
                    S.op("pe", _mm(bank(bG0), M[:, 1024 + c * 128:1024 + (c + 1) * 128], hT[:, c, 8:520], c == 0, c == 7),
                         reads=[bm_] + B_hT, writes=[B_bk[bG0]])
                for c in range(8):
                    S.op("pe", _mm(bank(bG1), M[:, 2048 + c * 128:2048 + (c + 1) * 128], hT[:, c, 8:520], c == 0, c == 7),
                         reads=[bm_] + B_hT, writes=[B_bk[bG1]])
                for g in range(4):
                    S.op("pe", _mm(bank(bPb), M[:, 3072 + g * 128:3072 + (g + 1) * 128], R[:, 28 + g, :], g == 0, g == 3),
                         reads=[bm_, B_R[28 + g]], writes=[B_bk[bPb]])
                t0, t1 = TB[2 * (j % 2)], TB[2 * (j % 2) + 1]
                bt0, bt1 = B_TB[2 * (j % 2)], B_TB[2 * (j % 2) + 1]
                S.op("act", _act(t0[:, 0:512], bank(bG0), AF.Sigmoid), reads=[B_bk[bG0]], writes=[bt0])
                S.op("act", _act(t1[:, 0:512], bank(bG1), AF.Sigmoid), reads=[B_bk[bG1]], writes=[bt1])
                S.op("dve", _tt(t0[:, 0:512], bank(bA), t0[:, 0:512], ALU.mult), reads=[B_bk[bA], bt0], writes=[bt0])
                S.op("dve", _tt(t1[:, 0:512], bank(bPb), t1[:, 0:512], ALU.mult), reads=[B_bk[bPb], bt1], writes=[bt1])
                S.op("pool", _tt(R[:, 20 + j, :], t0[:, 0:512], t1[:, 0:512], ALU.add),
                     reads=[bt0, bt1], writes=[B_R[20 + j]])
            if G == 0:
                dump_R("b6")
            so0 = ring_load(12)
            so1 = ring_load(13)
            wo = [slab3(so0), slab3(so1)]
            wob = [B_ring[so0], B_ring[so1]]

            def Ya(t):
                for hh in range(2):
                    for c in range(8):
                        S.op("pe", _mm(bank(2 * (t % 3) + hh), R[:, 20 + c, t * 128:(t + 1) * 128], wo[hh][:, c, :], c == 0, c == 7),
                             reads=[B_R[20 + c], wob[hh]], writes=[B_bk[2 * (t % 3) + hh]])

            def Yb(t):
                par = t % 2
                ybufs = [B_bk[2 * (t % 3)], B_bk[2 * (t % 3) + 1]]
                S.op("act", _act(qrot[par][:, :], PS[t % 3][:, :], AF.Square, accum_out=ssh[t][:, 0:1]),
                     reads=ybufs, writes=[B_ssh[t], B_qrot[par]])
                S.op("act", _act(sdh[t][:, 0:1], ssh[t][:, 0:1], AF.Sqrt, bias=epsT[:, :], scale=1.0 / D),
                     reads=[B_ssh[t], B_const], writes=[B_sdh[t]])
                S.op("dve", _recip(rstdh[t][:, 0:1], sdh[t][:, 0:1]), reads=[B_sdh[t]], writes=[B_rstdh[t]])

            def Yc(t):
                ybufs = [B_bk[2 * (t % 3)], B_bk[2 * (t % 3) + 1]]
                S.op("dve", _stt(TB[t][:, :], PS[t % 3][:, :], rstdh[t][:, 0:1], gpostmix[:, :], ALU.mult, ALU.mult),
                     reads=ybufs + [B_rstdh[t], B_const], writes=[B_TB[t]])
                S.op("dve", _tt(xres[:, t, :], xres[:, t, :], TB[t][:, :], ALU.add),
                     reads=[B_TB[t], B_x[t]], writes=[B_x[t]])

            modulo([Ya, Yb, Yc, Ha, Ha2, mk_Hb(gcolmlp)], 4)
            if G == 0 and dbg:
                S.dma("sp", _dma(dbg_x1, xres[:]), "dbg", reads=B_x, writes=[B_dbg])
            for s_ in range(8):
                si = ring_load(14 + s_)
                wi = slab3(si)
                for fl in range(4):
                    f = s_ * 4 + fl
                    bu = f % 4
                    for c in range(8):
                        S.op("pe", _mm(bank(bu), wi[:, c, fl * 128:(fl + 1) * 128], hT[:, c, 8:520], c == 0, c == 7),
                             reads=[B_ring[si]] + B_hT, writes=[B_bk[bu]])
                    tb = TB[bu]
                    S.op("act", _act(tb[:, 0:512], bank(bu), AF.Relu), reads=[B_bk[bu]], writes=[B_TB[bu]])
                    S.op("dve" if f % 4 != 3 else "pool", _tt(R[:, f, :], tb[:, 0:512], tb[:, 0:512], ALU.mult),
                         reads=[B_TB[bu]], writes=[B_R[f]])
            def Za(t):
                par = t % 2
                zb = [B_bk[2 * t], B_bk[2 * t + 1]]
                S.op("act", _act(qrot[par][:, :], PS[t][:, :], AF.Square, accum_out=ssh[t][:, 0:1]),
                     reads=zb, writes=[B_ssh[t], B_qrot[par]])
                S.op("act", _act(sdh[t][:, 0:1], ssh[t][:, 0:1], AF.Sqrt, bias=epsT[:, :], scale=1.0 / D),
                     reads=[B_ssh[t], B_const], writes=[B_sdh[t]])
                S.op("dve", _recip(rstdh[t][:, 0:1], sdh[t][:, 0:1]), reads=[B_sdh[t]], writes=[B_rstdh[t]])

            def Zb(t):
                zb = [B_bk[2 * t], B_bk[2 * t + 1]]
                S.op("dve", _stt(TB[t][:, :], PS[t][:, :], rstdh[t][:, 0:1], gpostmlp[:, :], ALU.mult, ALU.mult),
                     reads=zb + [B_rstdh[t], B_const], writes=[B_TB[t]])
                S.op("pool", _tt(TB[t][:, :], TB[t][:, :], xres[:, t, :], ALU.add),
                     reads=[B_TB[t], B_x[t]], writes=[B_TB[t]])
                S.dma("pool", _dma(out[r0 + t * 128:r0 + (t + 1) * 128, :], TB[t][:, :]), f"ost{t}",
                      reads=[B_TB[t]], writes=[B_out[t]])

            for hh in range(2):
                for blk in range(4):
                    so = ring_load(22 + hh * 4 + blk)
                    wmo_ = slab3(so)
                    last_slab = (hh == 1 and blk == 3)
                    order = ([(cc, t) for t in range(4) for cc in range(8)] if last_slab
                             else [(cc, t) for cc in range(8) for t in range(4)])
                    for (cc, t) in order:
                        f = blk * 8 + cc
                        S.op("pe", _mm(bank(2 * t + hh), R[:, f, t * 128:(t + 1) * 128], wmo_[:, cc, :], f == 0, f == 31),
                             reads=[B_R[f], B_ring[so]], writes=[B_bk[2 * t + hh]])
                        if last_slab and cc == 7:
                            Za(t)
                            if t >= 1:
                                Zb(t - 1)
            Zb(3)

        S.wait_all("sp", B_out + [B_dbg])

        eng_sems = {}
        for e in ENGS:
            for ep in range(S.n_epochs):
                eng_sems[(e, ep)] = st.enter_context(nc.semaphore(f"s_{e}_{ep}"))
        dma_sems = {k: st.enter_context(nc.semaphore(f"d_{k}")) for k in S.dma_keys()}
        block = st.enter_context(nc.Block())
        stats = S.emit(block, eng_sems, dma_sems)
        build_program.last_stats = stats
    return nc


def _rope_tables(real_idx):
    quarter = 32
    inv_freq = (10000.0 ** (-np.arange(quarter, dtype=np.float32) / np.float32(quarter))).astype(np.float32)
    idx = np.asarray(real_idx)
    valid = idx >= 0
    rows = np.where(valid, idx // 64, 0).astype(np.float32)
    cols = np.where(valid, idx % 64, 0).astype(np.float32)
    ang_r = (rows[:, None] * inv_freq[None, :]).astype(np.float32)
    ang_c = (cols[:, None] * inv_freq[None, :]).astype(np.float32)
    cr, sr, cc, sc = np.cos(ang_r), np.sin(ang_r), np.cos(ang_c), np.sin(ang_c)
    C = np.concatenate([cr, cr, cc, cc], axis=1)
    Sn = np.concatenate([-sr, sr, -sc, sc], axis=1)
    return np.concatenate([C, Sn], axis=1).astype(np.float32)


def _swap32(g):
    g4 = np.asarray(g, np.float32).reshape(2, 2, 32)
    return np.ascontiguousarray(g4[:, ::-1, :]).reshape(128)


def make_in_maps(x, meta_tokens, pre_mix_g, q_norm_g, k_norm_g, w_in, w_attn_br, w_pool_grp, pool_scale,
                 w_pool_br, w_out, post_mix_g, pre_mlp_g, w_mlp_in, w_mlp_out, post_mlp_g):
    f = np.float32
    x = np.asarray(x, f)
    meta = np.asarray(meta_tokens, f)
    shared = {
        "w_in": np.ascontiguousarray(np.asarray(w_in, f)[0]),
        "w_attn_br": np.ascontiguousarray(np.asarray(w_attn_br, f)[0]),
        "w_pool_grp": np.ascontiguousarray(np.asarray(w_pool_grp, f)[0]),
        "w_pool_br": np.ascontiguousarray(np.asarray(w_pool_br, f)[0]),
        "w_out": np.ascontiguousarray(np.asarray(w_out, f)[0]),
        "w_mlp_in": np.ascontiguousarray(np.asarray(w_mlp_in, f)[0]),
        "w_mlp_out": np.ascontiguousarray(np.asarray(w_mlp_out, f)[0]),
        "g_premix_c": np.ascontiguousarray(np.asarray(pre_mix_g, f)[0].reshape(8, 128).T),
        "g_premlp_c": np.ascontiguousarray(np.asarray(pre_mlp_g, f)[0].reshape(8, 128).T),
        "g_pscale_c": np.ascontiguousarray(np.asarray(pool_scale, f)[0].reshape(4, 128).T),
        "g_postmix": np.ascontiguousarray(np.asarray(post_mix_g, f)[0]),
        "g_postmlp": np.ascontiguousarray(np.asarray(post_mlp_g, f)[0]),
        "g_q": np.ascontiguousarray(np.asarray(q_norm_g, f)[0]),
        "g_qs": _swap32(np.asarray(q_norm_g, f)[0]),
        "g_k": np.ascontiguousarray(np.asarray(k_norm_g, f)[0]),
        "g_ks": _swap32(np.asarray(k_norm_g, f)[0]),
        "ident": np.eye(128, dtype=f).astype(ml_dtypes.bfloat16),
    }
    kidx = np.concatenate([np.full(128, -1), np.arange(SEQ)])
    ropeK = _rope_tables(kidx).reshape(NKT, 128, 256)
    shared["ropeK"] = ropeK
    in_maps = []
    for core in range(8):
        b, hf = core // 2, core % 2
        xkv = np.zeros((NKT * 128, D), f)
        xkv[0:N_META] = meta
        xkv[128:] = x[b]
        q0 = hf * NQ
        xq = np.zeros((NQ + 16, D), f)
        xq[8:8 + NQ] = x[b, q0:q0 + NQ]
        if hf == 0:
            xq[0:8] = meta[8:16]
            xq[8 + NQ:] = x[b, NQ:NQ + 8]
        else:
            xq[0:8] = x[b, q0 - 8:q0]
        ropeQ = _rope_tables(np.arange(q0, q0 + NQ)).reshape(NQ // 128, 128, 256)
        inv = np.zeros((NG, 4, 16), f)
        for G in range(NG):
            for g in range(4):
                w_ = 2 << g
                treal = q0 + G * GT + GT - 16 + np.arange(16)
                tpos = treal + N_META
                lo = np.clip(tpos - w_ // 2, 0, LTOT)
                hi = np.clip(tpos + w_ - w_ // 2, 0, LTOT)
                inv[G, g] = 1.0 / (hi - lo).astype(f)
        m = dict(shared)
        m["xkv"] = xkv
        m["xq"] = xq
        m["ropeQ"] = ropeQ
        m["invtail"] = inv.reshape(-1)
        in_maps.append(m)
    return in_maps


_NC_CACHE = {}


def kernel(**inputs):
    in_maps = make_in_maps(**inputs)
    if "nc" not in _NC_CACHE:
        _NC_CACHE["nc"] = build_program()
    nc = _NC_CACHE["nc"]
    res = run_bass_kernel_spmd(nc, in_maps, core_ids=list(range(8)))
    outp = np.empty((BATCH, SEQ, D), np.float32)
    for core in range(8):
        b, hf = core // 2, core % 2
        outp[b, hf * NQ:(hf + 1) * NQ] = np.asarray(res.results[core]["out"], np.float32)
    return outp
```

```python
import math
from contextlib import ExitStack

import numpy as np
import ml_dtypes
import concourse.bass as bass
import concourse.mybir as mybir
from concourse.bass_utils import run_bass_kernel_spmd

F32 = mybir.dt.float32
BF16 = mybir.dt.bfloat16
AF = mybir.ActivationFunctionType
ALU = mybir.AluOpType
AX = mybir.AxisListType

D = 1024
SEQ = 8192
BATCH = 4
N_META = 16
LTOT = SEQ + N_META
NQ = 4096
NG = 8
GT = 512
NKT = 65
EPS = 1e-6
NSLAB = 30
NSLOT = 4

ENGS = ("pe", "act", "dve", "pool", "sp")


class Buf:
    __slots__ = ("name", "last_w", "readers")

    def __init__(self, name):
        self.name = name
        self.last_w = None
        self.readers = {}


class Ins:
    __slots__ = ("eng", "fn", "deps", "needs_inc", "ticket", "dma_sem", "dma_val", "epoch")

    def __init__(self, eng, fn, epoch):
        self.eng = eng
        self.fn = fn
        self.deps = []
        self.needs_inc = False
        self.ticket = None
        self.dma_sem = None
        self.dma_val = None
        self.epoch = epoch


class Sched:
    def __init__(self):
        self.streams = {e: [] for e in ENGS}
        self.dma_counts = {}
        self.epoch = 0
        self.n_epochs = 1

    def next_epoch(self):
        self.epoch += 1
        self.n_epochs = self.epoch + 1

    def _collect(self, reads, writes):
        deps = []
        for b in reads:
            if b.last_w is not None:
                deps.append(b.last_w)
        for b in writes:
            if b.last_w is not None:
                deps.append(b.last_w)
            deps.extend(b.readers.values())
        return [("d", d[1], self.dma_counts[d[1]]) if d[0] == "d" else d for d in deps]

    def _finish(self, entry, key, reads, writes):
        for b in reads:
            b.readers[key] = entry
        for b in writes:
            b.last_w = entry
            b.readers = {}

    def op(self, eng, fn, reads=(), writes=()):
        ins = Ins(eng, fn, self.epoch)
        for d in self._collect(reads, writes):
            if d[0] == "c" and d[1].eng == eng and eng == "pe":
                continue
            ins.deps.append(d)
        self.streams[eng].append(ins)
        self._finish(("c", ins), eng, reads, writes)
        return ins

    def dma(self, queue, fn, semkey, reads=(), writes=()):
        ins = Ins(queue, fn, self.epoch)
        ins.deps = list(self._collect(reads, writes))
        c = self.dma_counts.get(semkey, 0) + 16
        self.dma_counts[semkey] = c
        ins.dma_sem = semkey
        ins.dma_val = c
        self.streams[queue].append(ins)
        self._finish(("d", semkey, c), "dma:" + semkey, reads, writes)
        return ins

    def wait_all(self, eng, bufs):
        ins = Ins(eng, None, self.epoch)
        ins.deps = list(self._collect((), bufs))
        self.streams[eng].append(ins)
        return ins

    def dma_keys(self):
        return list(self.dma_counts.keys())

    def emit(self, block, eng_sems, dma_sems):
        for e in ENGS:
            for ins in self.streams[e]:
                for d in ins.deps:
                    if d[0] == "c":
                        d[1].needs_inc = True
        for e in ENGS:
            t = {}
            for ins in self.streams[e]:
                if ins.needs_inc:
                    t[ins.epoch] = t.get(ins.epoch, 0) + 1
                    ins.ticket = t[ins.epoch]
        stats = {}

        def run(e, engobj):
            waited = {}
            nw = 0
            for ins in self.streams[e]:
                need = {}
                for d in ins.deps:
                    if d[0] == "c":
                        k, v = ("e", d[1].eng, d[1].epoch), d[1].ticket
                    else:
                        k, v = ("d", d[1]), d[2]
                    if waited.get(k, 0) >= v:
                        continue
                    if need.get(k, 0) < v:
                        need[k] = v
                for k, v in need.items():
                    sem = eng_sems[(k[1], k[2])] if k[0] == "e" else dma_sems[k[1]]
                    engobj.wait_ge(sem, v)
                    waited[k] = v
                    nw += 1
                if ins.fn is None:
                    continue
                bi = ins.fn(engobj)
                if ins.dma_sem is not None:
                    bi.then_inc(dma_sems[ins.dma_sem], 16)
                elif ins.needs_inc:
                    bi.then_inc(eng_sems[(e, ins.epoch)], 1)
            stats[e] = (len(self.streams[e]), nw)

        @block.tensor
        def _(eng):
            run("pe", eng)

        @block.scalar
        def _(eng):
            run("act", eng)

        @block.vector
        def _(eng):
            run("dve", eng)

        @block.gpsimd
        def _(eng):
            run("pool", eng)

        @block.sync
        def _(eng):
            run("sp", eng)

        return stats


def _mm(out, lhsT, rhs, start, stop):
    return lambda e: e.matmul(out, lhsT, rhs, start=start, stop=stop)


def _tr(out, in_, idn):
    return lambda e: e.transpose(out, in_, idn)


def _act(out, in_, func, **kw):
    return lambda e: e.activation(out=out, in_=in_, func=func, **kw)


def _tt(out, in0, in1, op):
    return lambda e: e.tensor_tensor(out=out, in0=in0, in1=in1, op=op)


def _ts(out, in0, s1, op0, s2=None, op1=None):
    if op1 is None:
        return lambda e: e.tensor_scalar(out=out, in0=in0, scalar1=s1, scalar2=None, op0=op0)
    return lambda e: e.tensor_scalar(out=out, in0=in0, scalar1=s1, scalar2=s2, op0=op0, op1=op1)


def _stt(out, in0, scalar, in1, op0, op1):
    return lambda e: e.scalar_tensor_tensor(out=out, in0=in0, scalar=scalar, in1=in1, op0=op0, op1=op1)


def _recip(out, in_):
    return lambda e: e.reciprocal(out=out, in_=in_)


def _red(out, in_):
    return lambda e: e.tensor_reduce(out=out, in_=in_, axis=AX.X, op=ALU.add)


def _dma(out, in_):
    return lambda e: e.dma_start(out=out, in_=in_)


def _copy(out, in_):
    return lambda e: e.tensor_copy(out=out, in_=in_)


def _memset(ap, v):
    return lambda e: e.memset(ap, v)


def build_program(dbg=False, n_groups=NG, n_kt=NKT):
    nc = bass.Bass("TRN2", target_bir_lowering=False)

    def din(name, shape, dt=F32):
        return nc.dram_tensor(name, list(shape), dt, kind="ExternalInput").ap()

    xkv = din("xkv", [NKT * 128, D])
    xq = din("xq", [NQ + 16, D])
    w_in = din("w_in", [D, 4096])
    w_attn = din("w_attn_br", [D, D])
    w_grp = din("w_pool_grp", [4, 128, 128])
    w_pbr = din("w_pool_br", [512, D])
    w_out = din("w_out", [D, D])
    w_mi = din("w_mlp_in", [D, 4096])
    w_mo = din("w_mlp_out", [4096, D])
    g_premix_c = din("g_premix_c", [128, 8])
    g_premlp_c = din("g_premlp_c", [128, 8])
    g_pscale_c = din("g_pscale_c", [128, 4])
    g_postmix = din("g_postmix", [D])
    g_postmlp = din("g_postmlp", [D])
    g_q = din("g_q", [128])
    g_qs = din("g_qs", [128])
    g_k = din("g_k", [128])
    g_ks = din("g_ks", [128])
    ropeK = din("ropeK", [NKT, 128, 256])
    ropeQ = din("ropeQ", [NQ // 128, 128, 256])
    invtail = din("invtail", [NG * 4 * 16])
    ident_d = din("ident", [128, 128], BF16)
    out = nc.dram_tensor("out", [NQ, D], F32, kind="ExternalOutput").ap()
    wsl = nc.dram_tensor("wsl", [NSLAB, 128, 4096], BF16, kind="Internal").ap()
    wgrp_d = nc.dram_tensor("wgrp_b", [128, 4, 128], BF16, kind="Internal").ap()
    if dbg:
        dbg_kt = nc.dram_tensor("dbg_kt", [128, 2, LTOT], BF16, kind="ExternalOutput").ap()
        dbg_v = nc.dram_tensor("dbg_v", [128, NKT, 256], BF16, kind="ExternalOutput").ap()
        dbg_R = {k: nc.dram_tensor("dbg_" + k, [128, 32, 512], BF16, kind="ExternalOutput").ap()
                 for k in ("b3", "b4", "b5", "b6")}
        dbg_hT = nc.dram_tensor("dbg_hT", [128, 8, 528], BF16, kind="ExternalOutput").ap()
        dbg_x1 = nc.dram_tensor("dbg_x1", [128, 4, D], F32, kind="ExternalOutput").ap()

    def dump_R(key):
        if dbg:
            S.dma("sp", _dma(dbg_R[key], R[:]), "dbg", reads=B_R, writes=[B_dbg])

    S = Sched()
    with ExitStack() as st:
        def sb(name, shape, dt):
            return st.enter_context(nc.sbuf_tensor(name, list(shape), dt))

        KT = sb("KT", [128, 2, LTOT], BF16)
        V = sb("V", [128, NKT, 256], BF16)
        R = sb("R", [128, 32, 512], BF16)
        xres = sb("xres", [128, 4, D], F32)
        xhalo = sb("xhalo", [16, D], F32)
        hT = sb("hT", [128, 8, 528], BF16)
        TB = [sb(f"TB{i}", [128, D], F32) for i in range(5)]
        hbf = [sb(f"hbf{i}", [128, D], BF16) for i in range(2)]
        qrot = [sb(f"qrot{i}", [128, D], BF16) for i in range(2)]
        pooled = [sb(f"pooled{i}", [128, 512], BF16) for i in range(4)]
        gpostmix = sb("gpostmix", [128, D], F32)
        gpostmlp = sb("gpostmlp", [128, D], F32)
        gcolmix = sb("gcolmix", [128, 8], F32)
        gcolmlp = sb("gcolmlp", [128, 8], F32)
        pscale = sb("pscale", [128, 4], F32)
        gq = sb("gq", [128, 128], F32)
        gqs = sb("gqs", [128, 128], F32)
        gk = sb("gk", [128, 128], F32)
        gks = sb("gks", [128, 128], F32)
        invt = sb("invt", [128, 64], F32)
        ident = sb("ident_sb", [128, 128], BF16)
        ones = sb("ones", [128, 128], BF16)
        sel32 = sb("sel32", [128, 128], F32)
        epsT = sb("epsT", [128, 1], F32)
        gmx = sb("gmx", [128, 2], F32)
        negc = sb("negc", [128, 1], F32)
        rtab = [sb(f"rtab{i}", [128, 256], F32) for i in range(2)]
        cs = [sb(f"cs{i}", [128, 256], F32) for i in range(2)]
        ss = [sb(f"ss{i}", [128, 8], F32) for i in range(4)]
        sd = [sb(f"sd{i}", [128, 8], F32) for i in range(4)]
        rstd = [sb(f"rstd{i}", [128, 8], F32) for i in range(4)]
        ssh = [sb(f"ssh{i}", [128, 8], F32) for i in range(4)]
        sdh = [sb(f"sdh{i}", [128, 8], F32) for i in range(4)]
        rstdh = [sb(f"rstdh{i}", [128, 8], F32) for i in range(4)]
        tail16 = sb("tail16", [128, 16], F32)
        wgrp = sb("wgrp", [128, 4, 128], BF16)
        ring = [sb(f"ring{i}", [128, 4096], BF16) for i in range(NSLOT)]
        PS = [st.enter_context(nc.psum_tensor(f"ps{i}", [128, 1024], F32)) for i in range(4)]

        def bank(b):
            return PS[b // 2][:, (b % 2) * 512:(b % 2) * 512 + 512]

        def bankb(b):
            return bank(b).bitcast(BF16)

        B_KT = [Buf(f"KT{i}") for i in range(NKT)]
        B_V = [Buf(f"V{i}") for i in range(NKT)]
        B_R = [Buf(f"R{i}") for i in range(32)]
        B_x = [Buf(f"x{i}") for i in range(4)]
        B_xh = Buf("xhalo")
        B_hT = [Buf(f"hT{i}") for i in range(4)]
        B_hTh = Buf("hTh")
        B_TB = [Buf(f"TB{i}") for i in range(5)]
        B_hbf = [Buf(f"hbf{i}") for i in range(2)]
        B_qrot = [Buf(f"qrot{i}") for i in range(2)]
        B_pooled = [Buf(f"pooled{i}") for i in range(4)]
        B_const = Buf("const")
        B_negc = Buf("negc")
        B_rtab = [Buf(f"rtab{i}") for i in range(2)]
        B_cs = [Buf(f"cs{i}") for i in range(2)]
        B_ss = [Buf(f"ss{i}") for i in range(4)]
        B_sd = [Buf(f"sd{i}") for i in range(4)]
        B_rstd = [Buf(f"rstd{i}") for i in range(4)]
        B_ssh = [Buf(f"ssh{i}") for i in range(4)]
        B_sdh = [Buf(f"sdh{i}") for i in range(4)]
        B_rstdh = [Buf(f"rstdh{i}") for i in range(4)]
        B_tail = Buf("tail16")
        B_invt = Buf("invt")
        B_wgrp = Buf("wgrp")
        B_wgrpd = Buf("wgrpd")
        B_ring = [Buf(f"ring{i}") for i in range(NSLOT)]
        B_ws = [Buf(f"ws{i}") for i in range(NSLAB)]
        B_bk = [Buf(f"bank{i}") for i in range(8)]
        B_out = [Buf(f"out{i}") for i in range(4)]
        B_dbg = Buf("dbg")

        B_cl = []

        def cload(dst, src):
            b_ = Buf("c%d" % len(B_cl))
            B_cl.append(b_)
            S.dma("sp", _dma(dst, src), "const", writes=[b_])

        cload(ident[:], ident_d)
        cload(gcolmix[:], g_premix_c)
        cload(gcolmlp[:], g_premlp_c)
        cload(pscale[:], g_pscale_c)
        cload(gk[:], g_k.partition_broadcast(128))
        cload(gks[:], g_ks.partition_broadcast(128))
        cload(gq[:], g_q.partition_broadcast(128))
        cload(gqs[:], g_qs.partition_broadcast(128))
        cload(gpostmix[:], g_postmix.partition_broadcast(128))
        cload(gpostmlp[:], g_postmlp.partition_broadcast(128))
        S.op("dve", _memset(ones[:], 1.0), writes=[B_const])
        S.op("dve", _memset(sel32[:], 1.0), writes=[B_const])
        S.op("dve", _memset(epsT[:], EPS), reads=B_cl, writes=[B_const])
        S.op("dve", lambda e: e.tensor_reduce(out=gmx[:, 0:1], in_=gq[:, :], axis=AX.X, op=ALU.max, apply_absolute_value=True),
             reads=[B_const], writes=[B_negc])
        S.op("dve", lambda e: e.tensor_reduce(out=gmx[:, 1:2], in_=gk[:, :], axis=AX.X, op=ALU.max, apply_absolute_value=True),
             reads=[B_const, B_negc], writes=[B_negc])
        S.op("dve", _tt(negc[:, :], gmx[:, 0:1], gmx[:, 1:2], ALU.mult), reads=[B_negc], writes=[B_negc])
        S.op("dve", _ts(negc[:, :], negc[:, :], -math.sqrt(128.0), ALU.mult), reads=[B_negc], writes=[B_negc])

        def cast(slab, pieces, key):
            for (src, c0, nch, ncols) in pieces:
                dst = wsl[slab][:, c0:c0 + nch * ncols].rearrange("p (c n) -> p c n", c=nch)
                srcv = src.rearrange("(c p) n -> p c n", p=128)
                S.dma("pool", _dma(dst, srcv), key, writes=[B_ws[slab]])

        cast(0, [(w_in[:, 1024:1536], 0, 8, 512)], "cast0")
        def early_casts():
            cast(1, [(w_in[:, 0:512], 0, 8, 512)], "castA")
            cast(2, [(w_in[:, 512:1024], 0, 8, 512)], "castA")
            cast(3, [(w_in[:, 1536:2048], 0, 8, 512)], "castA")
            S.dma("pool", _dma(wgrp_d, w_grp.rearrange("g c d -> c g d")), "castA", writes=[B_wgrpd])

        def deferred_casts():
            for j in range(8):
                cast(4 + j, [
                    (w_attn[:, j * 128:(j + 1) * 128], 0, 8, 128),
                    (w_in[:, 2048 + j * 128:2048 + (j + 1) * 128], 1024, 8, 128),
                    (w_in[:, 3072 + j * 128:3072 + (j + 1) * 128], 2048, 8, 128),
                    (w_pbr[:, j * 128:(j + 1) * 128], 3072, 4, 128),
                ], "castC")
            for hh in range(2):
                cast(12 + hh, [(w_out[:, hh * 512:(hh + 1) * 512], 0, 8, 512)], "castC")
            for s_ in range(8):
                cast(14 + s_, [(w_mi[:, s_ * 512:(s_ + 1) * 512], 0, 8, 512)], "castB")
            for hh in range(2):
                for blk in range(4):
                    cast(22 + hh * 4 + blk,
                         [(w_mo[blk * 1024:(blk + 1) * 1024, hh * 512:(hh + 1) * 512], 0, 8, 512)], "castB")

        ring_ctr = [0]

        def ring_load(slab, ncols=4096):
            slot = ring_ctr[0] % NSLOT
            ring_ctr[0] += 1
            S.dma("sp", _dma(ring[slot][:, 0:ncols], wsl[slab][:, 0:ncols]), f"ring{slot}",
                  reads=[B_ws[slab]], writes=[B_ring[slot]])
            return slot

        def slab3(slot):
            return ring[slot][:].rearrange("p (c n) -> p c n", c=8)

        class Ctx:
            pass

        def mk_ctx(i, tA, bA, tU, bU, rt, brt, cs_, bcs, rot, brot, hb, bhb):
            c = Ctx()
            c.ss, c.b_ss = ss[i], B_ss[i]
            c.sd, c.b_sd = sd[i], B_sd[i]
            c.rstd, c.b_rstd = rstd[i], B_rstd[i]
            c.ssh, c.b_ssh = ssh[i], B_ssh[i]
            c.sdh, c.b_sdh = sdh[i], B_sdh[i]
            c.rstdh, c.b_rstdh = rstdh[i], B_rstdh[i]
            c.tA, c.b_tA, c.tU, c.b_tU = tA, bA, tU, bU
            c.rtab, c.b_rtab, c.cs, c.b_cs = rt, brt, cs_, bcs
            c.rot, c.b_rot, c.hbf, c.b_hbf = rot, brot, hb, bhb
            return c

        ctxB = [mk_ctx(i, TB[i], [B_TB[i]], TB[2 + i], [B_TB[2 + i]], rtab[i], [B_rtab[i]], cs[i], [B_cs[i]],
                       qrot[i], [B_qrot[i]], hbf[i], [B_hbf[i]]) for i in range(2)]
        ctxA = []
        for i in range(4):
            ctxA.append(mk_ctx(
                i,
                R[:, i, :].bitcast(F32), [B_R[i]],
                R[:, 4 + i, :].bitcast(F32), [B_R[4 + i]],
                R[:, 12 + i, :].bitcast(F32), [B_R[12 + i]],
                R[:, 16 + i, :].bitcast(F32), [B_R[16 + i]],
                R[:, 8 + i, :], [B_R[8 + i]],
                R[:, 20 + 2 * i:22 + 2 * i, :].rearrange("p a n -> p (a n)"), [B_R[20 + 2 * i], B_R[21 + 2 * i]]))

        def emit_h(xap, xbuf, rows, cx, bT, gcol, dsts):
            S.op("act", _act(cx.hbf[0:rows, :], xap, AF.Square, accum_out=cx.ss[0:rows, 0:1]),
                 reads=[xbuf], writes=[cx.b_ss] + cx.b_hbf)
            S.op("act", _act(cx.sd[0:rows, 0:1], cx.ss[0:rows, 0:1], AF.Sqrt, bias=epsT[0:rows, :], scale=1.0 / D),
                 reads=[cx.b_ss, B_const], writes=[cx.b_sd])
            S.op("dve", _recip(cx.rstd[0:rows, 0:1], cx.sd[0:rows, 0:1]), reads=[cx.b_sd], writes=[cx.b_rstd])
            S.op("act", _act(cx.hbf[0:rows, :], xap, AF.Copy, scale=cx.rstd[0:rows, 0:1]),
                 reads=[xbuf, cx.b_rstd], writes=cx.b_hbf)
            pb = bankb(bT)
            for c in range(8):
                S.op("pe", _tr(pb[:, c * 128:c * 128 + rows], cx.hbf[0:rows, c * 128:(c + 1) * 128], ident[0:rows, 0:rows]),
                     reads=cx.b_hbf + [B_const], writes=[B_bk[bT]])
            pb3 = pb.rearrange("p (c n) -> p c n", c=8)
            for (hc0, pc0, ncol, dbuf) in dsts:
                S.op("dve", _tt(hT[:, :, hc0:hc0 + ncol], pb3[:, :, pc0:pc0 + ncol],
                                gcol[:].unsqueeze(2).broadcast_to([128, 8, ncol]), ALU.mult),
                     reads=[B_bk[bT], B_const], writes=[dbuf])

        def nr_square(X, xbank_bufs, H, cx):
            n = H * 128
            S.op("act", _act(cx.tA[:, 0:n], X, AF.Square), reads=xbank_bufs, writes=cx.b_tA)

        def nr_rest(X, xbank_bufs, H, cx, gt, gst):
            n = H * 128
            tA, tU = cx.tA, cx.tU
            S.op("dve", _red(cx.ssh[:, 0:H], tA[:, 0:n].rearrange("p (h d) -> p h d", h=H)),
                 reads=cx.b_tA, writes=[cx.b_ssh])
            S.op("act", _act(cx.sdh[:, 0:H], cx.ssh[:, 0:H], AF.Sqrt, bias=epsT[:, :], scale=1.0 / 128),
                 reads=[cx.b_ssh, B_const], writes=[cx.b_sdh])
            S.op("dve", _recip(cx.rstdh[:, 0:H], cx.sdh[:, 0:H]), reads=[cx.b_sdh], writes=[cx.b_rstdh])
            S.op("pool", _tt(cx.cs[:, 0:128], cx.rtab[:, 0:128], gt[:], ALU.mult),
                 reads=cx.b_rtab + [B_const], writes=cx.b_cs)
            S.op("pool", _tt(cx.cs[:, 128:256], cx.rtab[:, 128:256], gst[:], ALU.mult),
                 reads=cx.b_rtab + [B_const], writes=cx.b_cs)
            X5 = X.rearrange("p (h a b j) -> p h a b j", h=H, a=2, b=2, j=32)
            T5 = tA[:, 0:n].rearrange("p (h a b j) -> p h a b j", h=H, a=2, b=2, j=32)
            Sg = cx.cs[:, 128:256].rearrange("p (a b j) -> p a b j", a=2, b=2, j=32)
            for bsel in range(2):
                S.op("dve", _tt(T5[:, :, :, bsel, :], X5[:, :, :, 1 - bsel, :],
                                Sg[:, :, bsel, :].unsqueeze(1).broadcast_to([128, H, 2, 32]), ALU.mult),
                     reads=xbank_bufs + cx.b_cs, writes=cx.b_tA)
            S.op("dve", _tt(tU[:, 0:n].rearrange("p (h d) -> p h d", h=H), X.rearrange("p (h d) -> p h d", h=H),
                            cx.cs[:, 0:128].unsqueeze(1).broadcast_to([128, H, 128]), ALU.mult),
                 reads=xbank_bufs + cx.b_cs, writes=cx.b_tU)
            S.op("pool", _tt(tU[:, 0:n], tU[:, 0:n], tA[:, 0:n], ALU.add),
                 reads=cx.b_tA + cx.b_tU, writes=cx.b_tU)
            S.op("dve", _tt(cx.rot[:, 0:n].rearrange("p (h d) -> p h d", h=H), tU[:, 0:n].rearrange("p (h d) -> p h d", h=H),
                            cx.rstdh[:, 0:H].unsqueeze(2).broadcast_to([128, H, 128]), ALU.mult),
                 reads=cx.b_tU + [cx.b_rstdh], writes=cx.b_rot)

        kvslot = ring_load(0)
        wkv = slab3(kvslot)

        def a_ctx(kt):
            return ctxA[kt % 4]

        def sqbuf(kt):
            return R[:, 28 + kt % 4, :].bitcast(F32), [B_R[28 + kt % 4]]

        def S1(kt):
            q = kt % 4
            cx = a_ctx(kt)
            xap, xbuf = xres[:, q, :], B_x[q]
            S.dma("sp", _dma(xap, xkv[kt * 128:(kt + 1) * 128, :]), f"xa{q}", writes=[xbuf])
            S.dma("sp", _dma(cx.rtab, ropeK[kt]), f"rt{q}", writes=cx.b_rtab)
            S.op("act", _act(cx.hbf[:, :], xap, AF.Square, accum_out=cx.ss[:, 0:1]), reads=[xbuf],
                 writes=[cx.b_ss] + cx.b_hbf)
            S.op("act", _act(cx.sd[:, 0:1], cx.ss[:, 0:1], AF.Sqrt, bias=epsT[:, :], scale=1.0 / D),
                 reads=[cx.b_ss, B_const], writes=[cx.b_sd])
            S.op("dve", _recip(cx.rstd[:, 0:1], cx.sd[:, 0:1]), reads=[cx.b_sd], writes=[cx.b_rstd])

        def S1b(kt):
            q = kt % 4
            cx = a_ctx(kt)
            xap, xbuf = xres[:, q, :], B_x[q]
            S.op("act", _act(cx.hbf[:, :], xap, AF.Copy, scale=cx.rstd[:, 0:1]),
                 reads=[xbuf, cx.b_rstd], writes=cx.b_hbf)

        def S2(kt):
            q = kt % 4
            cx = a_ctx(kt)
            bT, bkv = kt % 3, 3 + kt % 3
            pb = bankb(bT)
            for c in range(8):
                S.op("pe", _tr(pb[:, c * 128:(c + 1) * 128], cx.hbf[:, c * 128:(c + 1) * 128], ident[:, :]),
                     reads=cx.b_hbf + [B_const], writes=[B_bk[bT]])
            pb3 = pb.rearrange("p (c n) -> p c n", c=8)
            S.op("dve", _tt(hT[:, :, 8 + q * 128:8 + (q + 1) * 128], pb3[:, :, :],
                            gcolmix[:].unsqueeze(2).broadcast_to([128, 8, 128]), ALU.mult),
                 reads=[B_bk[bT], B_const], writes=[B_hT[q]])
            for c in range(8):
                S.op("pe", _mm(bank(bkv), hT[:, c, 8 + q * 128:8 + (q + 1) * 128], wkv[:, c, :], c == 0, c == 7),
                     reads=[B_hT[q], B_ring[kvslot]], writes=[B_bk[bkv]])

        def S3(kt):
            cx = a_ctx(kt)
            bkv = 3 + kt % 3
            X = bank(bkv)[:, 0:256]
            xb_ = [B_bk[bkv]]
            sq, bsq = sqbuf(kt)
            S.op("dve", _copy(V[:, kt, :], bank(bkv)[:, 256:512]), reads=xb_, writes=[B_V[kt]])
            S.op("act", _act(sq[:, :], X, AF.Square), reads=xb_, writes=bsq)
            S.op("pool", _tt(cx.cs[:, 0:128], cx.rtab[:, 0:128], gk[:], ALU.mult),
                 reads=cx.b_rtab + [B_const], writes=cx.b_cs)
            S.op("pool", _tt(cx.cs[:, 128:256], cx.rtab[:, 128:256], gks[:], ALU.mult),
                 reads=cx.b_rtab + [B_const], writes=cx.b_cs)
            S.op("dve", _red(cx.ssh[:, 0:2], sq[:, :].rearrange("p (h d) -> p h d", h=2)),
                 reads=bsq, writes=[cx.b_ssh])
            X5 = X.rearrange("p (h a b j) -> p h a b j", h=2, a=2, b=2, j=32)
            T5 = cx.tA[:, :].rearrange("p (h a b j) -> p h a b j", h=2, a=2, b=2, j=32)
            Sg = cx.cs[:, 128:256].rearrange("p (a b j) -> p a b j", a=2, b=2, j=32)
            for bsel in range(2):
                S.op("dve", _tt(T5[:, :, :, bsel, :], X5[:, :, :, 1 - bsel, :],
                                Sg[:, :, bsel, :].unsqueeze(1).broadcast_to([128, 2, 2, 32]), ALU.mult),
                     reads=xb_ + cx.b_cs, writes=cx.b_tA)
            S.op("dve", _tt(cx.tU[:, :].rearrange("p (h d) -> p h d", h=2), X.rearrange("p (h d) -> p h d", h=2),
                            cx.cs[:, 0:128].unsqueeze(1).broadcast_to([128, 2, 128]), ALU.mult),
                 reads=xb_ + cx.b_cs, writes=cx.b_tU)

        def S4(kt):
            cx = a_ctx(kt)
            S.op("act", _act(cx.sdh[:, 0:2], cx.ssh[:, 0:2], AF.Sqrt, bias=epsT[:, :], scale=1.0 / 128),
                 reads=[cx.b_ssh, B_const], writes=[cx.b_sdh])
            S.op("dve", _recip(cx.rstdh[:, 0:2], cx.sdh[:, 0:2]), reads=[cx.b_sdh], writes=[cx.b_rstdh])
            S.op("pool", _tt(cx.tU[:, :], cx.tU[:, :], cx.tA[:, :], ALU.add),
                 reads=cx.b_tA + cx.b_tU, writes=cx.b_tU)

        def S4b(kt):
            cx = a_ctx(kt)
            S.op("dve", _tt(cx.rot[:, 0:256].rearrange("p (h d) -> p h d", h=2), cx.tU[:, :].rearrange("p (h d) -> p h d", h=2),
                            cx.rstdh[:, 0:2].unsqueeze(2).broadcast_to([128, 2, 128]), ALU.mult),
                 reads=cx.b_tU + [cx.b_rstdh], writes=cx.b_rot)

        def S5(kt):
            cx = a_ctx(kt)
            bkt = 6 + kt % 2
            pb = bankb(bkt)
            for h in range(2):
                S.op("pe", _tr(pb[:, h * 128:(h + 1) * 128], cx.rot[:, h * 128:(h + 1) * 128], ident[:]),
                     reads=cx.b_rot + [B_const], writes=[B_bk[bkt]])
            ntok = 16 if kt == 0 else 128
            col0 = 0 if kt == 0 else 16 + (kt - 1) * 128
            S.op("dve", _copy(KT[:, :, col0:col0 + ntok], pb.rearrange("p (c n) -> p c n", c=8)[:, 0:2, 0:ntok]),
                 reads=[B_bk[bkt]], writes=[B_KT[kt]])

        stages = [S1, S1b, S2, S3, S4, S4b, S5]
        for step in range(n_kt + len(stages) - 1):
            if step == min(8, n_kt - 1):
                early_casts()
            for si, fn in enumerate(stages):
                k_ = step - si
                if 0 <= k_ < n_kt:
                    fn(k_)

        if dbg:
            S.dma("sp", _dma(dbg_kt, KT[:]), "dbg", reads=B_KT, writes=[B_dbg])
            S.dma("sp", _dma(dbg_v, V[:]), "dbg", reads=B_V, writes=[B_dbg])

        scale_qk = 1.0 / math.sqrt(128.0)

        def modulo(stage_fns, n):
            for step in range(n + len(stage_fns) - 1):
                for si in reversed(range(len(stage_fns))):
                    t_ = step - si
                    if 0 <= t_ < n:
                        stage_fns[si](t_)

        for G in range(n_groups):
            S.next_epoch()
            r0 = G * GT
            sq0 = ring_load(1)
            sq1 = ring_load(2)
            wq = [slab3(sq0), slab3(sq1)]
            wqb = [B_ring[sq0], B_ring[sq1]]
            S.dma("sp", _dma(xhalo[0:8, :], xq[r0:r0 + 8, :]), "xlh", writes=[B_xh])
            S.dma("sp", _dma(xhalo[8:16, :], xq[8 + r0 + GT:8 + r0 + GT + 8, :]), "xlh", writes=[B_xh])
            for t in range(4):
                S.dma("sp", _dma(xres[:, t, :], xq[8 + r0 + t * 128:8 + r0 + (t + 1) * 128, :]), f"xl{t}",
                      writes=[B_x[t]])
            S.dma("sp", _dma(invt[:], invtail[G * 64:(G + 1) * 64].partition_broadcast(128)), "invt", writes=[B_invt])

            def Ha(t, gcol_unused=None):
                cx = ctxB[t % 2]
                xap, xbuf = xres[:, t, :], B_x[t]
                S.op("act", _act(cx.hbf[:, :], xap, AF.Square, accum_out=cx.ss[:, 0:1]),
                     reads=[xbuf], writes=[cx.b_ss] + cx.b_hbf)
                S.op("act", _act(cx.sd[:, 0:1], cx.ss[:, 0:1], AF.Sqrt, bias=epsT[:, :], scale=1.0 / D),
                     reads=[cx.b_ss, B_const], writes=[cx.b_sd])
                S.op("dve", _recip(cx.rstd[:, 0:1], cx.sd[:, 0:1]), reads=[cx.b_sd], writes=[cx.b_rstd])

            def Ha2(t):
                cx = ctxB[t % 2]
                xap, xbuf = xres[:, t, :], B_x[t]
                S.op("act", _act(cx.hbf[:, :], xap, AF.Copy, scale=cx.rstd[:, 0:1]),
                     reads=[xbuf, cx.b_rstd], writes=cx.b_hbf)

            def mk_Hb(gcol):
                def Hb(t):
                    cx = ctxB[t % 2]
                    bT = 6 + t % 2
                    pb = bankb(bT)
                    for c in range(8):
                        S.op("pe", _tr(pb[:, c * 128:(c + 1) * 128], cx.hbf[:, c * 128:(c + 1) * 128], ident[:, :]),
                             reads=cx.b_hbf + [B_const], writes=[B_bk[bT]])
                    S.op("dve", _tt(hT[:, :, 8 + t * 128:8 + (t + 1) * 128], pb.rearrange("p (c n) -> p c n", c=8),
                                    gcol[:].unsqueeze(2).broadcast_to([128, 8, 128]), ALU.mult),
                         reads=[B_bk[bT], B_const], writes=[B_hT[t]])
                return Hb

            def Hh():
                emit_h(xhalo[:, :], B_xh, 16, ctxB[0], 6, gcolmix, [(0, 0, 8, B_hTh), (520, 8, 8, B_hTh)])

            def Qa(t):
                par = t % 2
                S.dma("sp", _dma(rtab[par][:], ropeQ[G * 4 + t]), f"rq{par}", writes=[B_rtab[par]])
                for hh in range(2):
                    for c in range(8):
                        S.op("pe", _mm(bank(2 * par + hh), hT[:, c, 8 + t * 128:8 + (t + 1) * 128], wq[hh][:, c, :], c == 0, c == 7),
                             reads=[B_hT[t], wqb[hh]], writes=[B_bk[2 * par + hh]])
                S.op("act", _act(TB[4][:, :], PS[par][:, :], AF.Square),
                     reads=[B_bk[2 * par], B_bk[2 * par + 1]], writes=[B_TB[4]])

            def Qb(t):
                par = t % 2
                cx = ctxB[par]
                X = PS[par][:, :]
                xb_ = [B_bk[2 * par], B_bk[2 * par + 1]]
                S.op("dve", _red(cx.ssh[:, 0:8], TB[4][:, :].rearrange("p (h d) -> p h d", h=8)),
                     reads=[B_TB[4]], writes=[cx.b_ssh])
                S.op("pool", _tt(cx.cs[:, 0:128], cx.rtab[:, 0:128], gq[:], ALU.mult),
                     reads=cx.b_rtab + [B_const], writes=cx.b_cs)
                S.op("pool", _tt(cx.cs[:, 128:256], cx.rtab[:, 128:256], gqs[:], ALU.mult),
                     reads=cx.b_rtab + [B_const], writes=cx.b_cs)
                X5 = X.rearrange("p (h a b j) -> p h a b j", h=8, a=2, b=2, j=32)
                T5 = cx.tA[:, :].rearrange("p (h a b j) -> p h a b j", h=8, a=2, b=2, j=32)
                Sg = cx.cs[:, 128:256].rearrange("p (a b j) -> p a b j", a=2, b=2, j=32)
                for bsel in range(2):
                    S.op("dve", _tt(T5[:, :, :, bsel, :], X5[:, :, :, 1 - bsel, :],
                                    Sg[:, :, bsel, :].unsqueeze(1).broadcast_to([128, 8, 2, 32]), ALU.mult),
                         reads=xb_ + cx.b_cs, writes=cx.b_tA)
                S.op("dve", _tt(cx.tU[:, :].rearrange("p (h d) -> p h d", h=8), X.rearrange("p (h d) -> p h d", h=8),
                                cx.cs[:, 0:128].unsqueeze(1).broadcast_to([128, 8, 128]), ALU.mult),
                     reads=xb_ + cx.b_cs, writes=cx.b_tU)

            def Qc(t):
                cx = ctxB[t % 2]
                S.op("act", _act(cx.sdh[:, 0:8], cx.ssh[:, 0:8], AF.Sqrt, bias=epsT[:, :], scale=1.0 / 128),
                     reads=[cx.b_ssh, B_const], writes=[cx.b_sdh])
                S.op("dve", _recip(cx.rstdh[:, 0:8], cx.sdh[:, 0:8]), reads=[cx.b_sdh], writes=[cx.b_rstdh])
                S.op("pool", _tt(cx.tU[:, :], cx.tU[:, :], cx.tA[:, :], ALU.add),
                     reads=cx.b_tA + cx.b_tU, writes=cx.b_tU)

            def Qc2(t):
                cx = ctxB[t % 2]
                S.op("pool", _tt(cx.rot[:, :].rearrange("p (h d) -> p h d", h=8), cx.tU[:, :].rearrange("p (h d) -> p h d", h=8),
                                cx.rstdh[:, 0:8].unsqueeze(2).broadcast_to([128, 8, 128]), ALU.mult),
                     reads=cx.b_tU + [cx.b_rstdh], writes=cx.b_rot)

            def Qd(t):
                par = t % 2
                cx = ctxB[par]
                bq = 4 + par
                pb = bankb(bq)
                for h in range(8):
                    S.op("pe", _tr(pb[:, h * 128:(h + 1) * 128], cx.rot[:, h * 128:(h + 1) * 128], ident[:]),
                         reads=cx.b_rot + [B_const], writes=[B_bk[bq]])
                S.op("act", _act(R[:, 0:8, t * 128:(t + 1) * 128], pb.rearrange("p (c n) -> p c n", c=8), AF.Copy),
                     reads=[B_bk[bq]], writes=B_R[0:8])

            Hh()
            modulo([Ha, Ha2, mk_Hb(gcolmix), Qa, Qb, Qc, Qc2, Qd], 4)
            if G == 0:
                dump_R("b3")
                if dbg:
                    S.dma("sp", _dma(dbg_hT, hT[:]), "dbg", reads=B_hT + [B_hTh], writes=[B_dbg])
            sp_ = ring_load(3)
            wp = slab3(sp_)
            for g in range(4):
                par = g % 2
                w_ = 2 << g
                bm, bh = par, 2 + par
                for c in range(8):
                    S.op("pe", _mm(bank(bm), wp[:, c, g * 128:(g + 1) * 128], hT[:, c, 8:520], c == 0, c == 7),
                         reads=B_hT + [B_ring[sp_]], writes=[B_bk[bm]])
                for c in range(8):
                    S.op("pe", _mm(bank(bh)[:, 0:8], wp[:, c, g * 128:(g + 1) * 128], hT[:, c, 0:8], c == 0, c == 7),
                         reads=[B_hTh, B_ring[sp_]], writes=[B_bk[bh]])
                for c in range(8):
                    S.op("pe", _mm(bank(bh)[:, 8:16], wp[:, c, g * 128:(g + 1) * 128], hT[:, c, 520:528], c == 0, c == 7),
                         reads=[B_hTh, B_ring[sp_]], writes=[B_bk[bh]])
                pbuf = TB[par]
                S.op("act", _act(pbuf[:, 8:520], bank(bm), AF.Copy), reads=[B_bk[bm]], writes=[B_TB[par]])
                S.op("act", _act(pbuf[:, 0:8], bank(bh)[:, 0:8], AF.Copy), reads=[B_bk[bh]], writes=[B_TB[par]])
                S.op("act", _act(pbuf[:, 520:528], bank(bh)[:, 8:16], AF.Copy), reads=[B_bk[bh]], writes=[B_TB[par]])
                s2 = TB[2 + par]
                s4 = TB[4]
                S.op("pool", _tt(s2[:, 1:528], pbuf[:, 0:527], pbuf[:, 1:528], ALU.add),
                     reads=[B_TB[par]], writes=[B_TB[2 + par]])
                ssum, bsum = s2, B_TB[2 + par]
                if g >= 1:
                    S.op("pool", _tt(s4[:, 2:527], s2[:, 1:526], s2[:, 3:528], ALU.add),
                         reads=[B_TB[2 + par]], writes=[B_TB[4]])
                    ssum, bsum = s4, B_TB[4]
                if g >= 2:
                    S.op("pool", _tt(s2[:, 4:525], s4[:, 2:523], s4[:, 6:527], ALU.add),
                         reads=[B_TB[4]], writes=[B_TB[2 + par]])
                    ssum, bsum = s2, B_TB[2 + par]
                if g >= 3:
                    S.op("pool", _tt(s4[:, 8:521], s2[:, 4:517], s2[:, 12:525], ALU.add),
                         reads=[B_TB[2 + par]], writes=[B_TB[4]])
                    ssum, bsum = s4, B_TB[4]
                S.op("dve", _stt(pooled[g][:, :], ssum[:, 8:520], 1.0 / w_, pbuf[:, 8:520], ALU.mult, ALU.subtract),
                     reads=[bsum, B_TB[par]], writes=[B_pooled[g]])
                io = g * 16
                S.op("dve", _tt(tail16[:, :], ssum[:, 504:520], invt[:, io:io + 16], ALU.mult),
                     reads=[bsum, B_invt], writes=[B_tail])
                S.op("dve", _tt(pooled[g][:, 496:512], tail16[:, :], pbuf[:, 504:520], ALU.subtract),
                     reads=[B_tail, B_TB[par]], writes=[B_pooled[g]])
            if G == 0:
                deferred_casts()
            NP = 6
            PE_EVERY = 4
            npairs = (n_kt - 1) // 2
            units = [[1 + 2 * j, 2 + 2 * j] for j in range(npairs)] + [[0]]
            ntiles = sum(len(u_) for u_ in units)
            pending_fin = [None]
            uctr = [0]

            def fin_a(h):
                aD, bD = TB[h % 2], B_TB[h % 2]
                S.op("dve", _copy(TB[2 + h % 2][:, 0:512], bank(6)), reads=[B_bk[6]], writes=[B_TB[2 + h % 2]])
                S.op("dve", _tt(TB[4][:, 0:512], aD[:, 0:512], aD[:, 512:1024], ALU.add), reads=[bD], writes=[B_TB[4]])

            def fin_b(h):
                bo, btot = 6, 7
                S.op("pe", _mm(bank(btot), sel32[:, :], TB[4][:, 0:512], False, True),
                     reads=[B_TB[4], B_const], writes=[B_bk[btot]])
                S.op("dve", _recip(TB[4][:, 512:1024], bank(btot)), reads=[B_bk[btot]], writes=[B_TB[4]])
                S.op("dve", _tt(R[:, 8 + h, :], TB[2 + h % 2][:, 0:512], TB[4][:, 512:1024], ALU.mult),
                     reads=[B_TB[2 + h % 2], B_TB[4]], writes=[B_R[8 + h]])

            for h in range(8):
                kv = h // 4
                bo, btot = 6, 7
                aD, bD = TB[h % 2], B_TB[h % 2]
                npe = [0]
                tc = [0]
                ninit = {"dve": 0, "pool": 0}

                def qk(uu, unit):
                    for idx, kt in enumerate(unit):
                        nk = 16 if kt == 0 else 128
                        col0 = 0 if kt == 0 else 16 + (kt - 1) * 128
                        S.op("pe", _mm(PS[uu % 3][0:nk, idx * 512:(idx + 1) * 512], KT[:, kv, col0:col0 + nk], R[:, h, :], True, True),
                             reads=[B_KT[kt], B_R[h]], writes=[B_bk[2 * (uu % 3) + idx]])

                def rest(uu, unit, pi):
                    nk = 16 if unit[0] == 0 else 128
                    w = 512 * len(unit)
                    g0 = 16 + 2 * (uu % NP)
                    ptb = R[0:nk, g0:g0 + 2, :].rearrange("p a n -> p (a n)")
                    pbufs = [B_R[g0], B_R[g0 + 1]][0:len(unit)]
                    sbufs = [B_bk[2 * (uu % 3) + i_] for i_ in range(len(unit))]
                    S.op("act", _act(ptb[:, 0:w], PS[uu % 3][0:nk, 0:w], AF.Exp, scale=scale_qk, bias=negc[0:nk, :]),
                         reads=sbufs + [B_negc], writes=pbufs)
                    for idx, kt in enumerate(unit):
                        first = (tc[0] + idx == 0)
                        last = (tc[0] + idx == ntiles - 1)
                        S.op("pe", _mm(bank(bo), V[0:nk, kt, kv * 128:(kv + 1) * 128], ptb[:, idx * 512:(idx + 1) * 512], first, last),
                             reads=[B_V[kt], pbufs[idx]], writes=[B_bk[bo]])
                    tc[0] += len(unit)
                    if unit[0] == 0:
                        S.op("dve", _tt(aD[0:nk, 0:512], aD[0:nk, 0:512], ptb[:, 0:512], ALU.add),
                             reads=pbufs + [bD], writes=[bD])
                        return
                    if pi % PE_EVERY == PE_EVERY - 1:
                        for idx in range(2):
                            S.op("pe", _mm(bank(btot), ones[:, :], ptb[:, idx * 512:(idx + 1) * 512], npe[0] == 0, False),
                                 reads=[B_const, pbufs[idx]], writes=[B_bk[btot]])
                            npe[0] += 1
                        return
                    if ninit["dve"] == 0:
                        S.op("dve", _copy(aD[:, :], ptb[:, :]), reads=pbufs, writes=[bD])
                    else:
                        S.op("dve", _tt(aD[:, :], aD[:, :], ptb[:, :], ALU.add), reads=pbufs + [bD], writes=[bD])
                    ninit["dve"] += 1

                u0 = uctr[0]
                AH = 2
                for u in range(min(AH, len(units))):
                    qk(u0 + u, units[u])
                for u in range(len(units)):
                    if u + AH < len(units):
                        qk(u0 + u + AH, units[u + AH])
                    rest(u0 + u, units[u], u)
                    if u == 1 and pending_fin[0] is not None:
                        fin_b(pending_fin[0])
                        pending_fin[0] = None
                uctr[0] += len(units)
                fin_a(h)
                pending_fin[0] = h
            fin_b(pending_fin[0])
            if G == 0:
                S.dma("sp", _dma(wgrp[:], wgrp_d), "wgrp", reads=[B_wgrpd], writes=[B_wgrp])
            for g in range(4):
                bg = g % 3
                S.op("pe", _mm(bank(bg), wgrp[:, g, :], pooled[g][:, :], True, True),
                     reads=[B_wgrp, B_pooled[g]], writes=[B_bk[bg]])
                S.op("act", _act(R[:, 28 + g, :], bank(bg), AF.Copy, scale=pscale[:, g:g + 1]),
                     reads=[B_bk[bg], B_const], writes=[B_R[28 + g]])
            if G == 0:
                dump_R("b5")
            for j in range(8):
                sm = ring_load(4 + j, 3584)
                M = ring[sm]
                bm_ = B_ring[sm]
                base = 0 if j % 2 == 0 else 4
                bA, bG0, bG1, bPb = base, base + 1, base + 2, base + 3
                for c in range(8):
                    S.op("pe", _mm(bank(bA), M[:, c * 128:(c + 1) * 128], R[:, 8 + c, :], c == 0, c == 7),
                         reads=[bm_, B_R[8 + c]], writes=[B_bk[bA]])
                for c in range(8):
                    S.op("pe", _mm(bank(bG0), M[:, 1024 + c * 128:1024 + (c + 1) * 128], hT[:, c, 8:520], c == 0, c == 7),
                         reads=[bm_] + B_hT, writes=[B_bk[bG0]])
                for c in range(8):
                    S.op("pe", _mm(bank(bG1), M[:, 2048 + c * 128:2048 + (c + 1) * 128], hT[:, c, 8:520], c == 0, c == 7),
                         reads=[bm_] + B_hT, writes=[B_bk[bG1]])
                for g in range(4):
                    S.op("pe", _mm(bank(bPb), M[:, 3072 + g * 128:3072 + (g + 1) * 128], R[:, 28 + g, :], g == 0, g == 3),
                         reads=[bm_, B_R[28 + g]], writes=[B_bk[bPb]])
                t0, t1 = TB[2 * (j % 2)], TB[2 * (j % 2) + 1]
                bt0, bt1 = B_TB[2 * (j % 2)], B_TB[2 * (j % 2) + 1]
                S.op("act", _act(t0[:, 0:512], bank(bG0), AF.Sigmoid), reads=[B_bk[bG0]], writes=[bt0])
                S.op("act", _act(t1[:, 0:512], bank(bG1), AF.Sigmoid), reads=[B_bk[bG1]], writes=[bt1])
                S.op("dve", _tt(t0[:, 0:512], bank(bA), t0[:, 0:512], ALU.mult), reads=[B_bk[bA], bt0], writes=[bt0])
                S.op("dve", _tt(t1[:, 0:512], bank(bPb), t1[:, 0:512], ALU.mult), reads=[B_bk[bPb], bt1], writes=[bt1])
                S.op("pool", _tt(R[:, 20 + j, :], t0[:, 0:512], t1[:, 0:512], ALU.add),
                     reads=[bt0, bt1], writes=[B_R[20 + j]])
            if G == 0:
                dump_R("b6")
            so0 = ring_load(12)
            so1 = ring_load(13)
            wo = [slab3(so0), slab3(so1)]
            wob = [B_ring[so0], B_ring[so1]]

            def Ya(t):
                for hh in range(2):
                    for c in range(8):
                        S.op("pe", _mm(bank(2 * (t % 3) + hh), R[:, 20 + c, t * 128:(t + 1) * 128], wo[hh][:, c, :], c == 0, c == 7),
                             reads=[B_R[20 + c], wob[hh]], writes=[B_bk[2 * (t % 3) + hh]])

            def Yb(t):
                par = t % 2
                ybufs = [B_bk[2 * (t % 3)], B_bk[2 * (t % 3) + 1]]
                S.op("act", _act(qrot[par][:, :], PS[t % 3][:, :], AF.Square, accum_out=ssh[t][:, 0:1]),
                     reads=ybufs, writes=[B_ssh[t], B_qrot[par]])
                S.op("act", _act(sdh[t][:, 0:1], ssh[t][:, 0:1], AF.Sqrt, bias=epsT[:, :], scale=1.0 / D),
                     reads=[B_ssh[t], B_const], writes=[B_sdh[t]])
                S.op("dve", _recip(rstdh[t][:, 0:1], sdh[t][:, 0:1]), reads=[B_sdh[t]], writes=[B_rstdh[t]])

            def Yc(t):
                ybufs = [B_bk[2 * (t % 3)], B_bk[2 * (t % 3) + 1]]
                S.op("dve", _stt(TB[t][:, :], PS[t % 3][:, :], rstdh[t][:, 0:1], gpostmix[:, :], ALU.mult, ALU.mult),
                     reads=ybufs + [B_rstdh[t], B_const], writes=[B_TB[t]])
                S.op("dve", _tt(xres[:, t, :], xres[:, t, :], TB[t][:, :], ALU.add),
                     reads=[B_TB[t], B_x[t]], writes=[B_x[t]])

            modulo([Ya, Yb, Yc, Ha, Ha2, mk_Hb(gcolmlp)], 4)
            if G == 0 and dbg:
                S.dma("sp", _dma(dbg_x1, xres[:]), "dbg", reads=B_x, writes=[B_dbg])
            for s_ in range(8):
                si = ring_load(14 + s_)
                wi = slab3(si)
                for fl in range(4):
                    f = s_ * 4 + fl
                    bu = f % 4
                    for c in range(8):
                        S.op("pe", _mm(bank(bu), wi[:, c, fl * 128:(fl + 1) * 128], hT[:, c, 8:520], c == 0, c == 7),
                             reads=[B_ring[si]] + B_hT, writes=[B_bk[bu]])
                    tb = TB[bu]
                    S.op("act", _act(tb[:, 0:512], bank(bu), AF.Relu), reads=[B_bk[bu]], writes=[B_TB[bu]])
                    S.op("dve" if f % 4 != 3 else "pool", _tt(R[:, f, :], tb[:, 0:512], tb[:, 0:512], ALU.mult),
                         reads=[B_TB[bu]], writes=[B_R[f]])
            def Za(t):
                par = t % 2
                zb = [B_bk[2 * t], B_bk[2 * t + 1]]
                S.op("act", _act(qrot[par][:, :], PS[t][:, :], AF.Square, accum_out=ssh[t][:, 0:1]),
                     reads=zb, writes=[B_ssh[t], B_qrot[par]])
                S.op("act", _act(sdh[t][:, 0:1], ssh[t][:, 0:1], AF.Sqrt, bias=epsT[:, :], scale=1.0 / D),
                     reads=[B_ssh[t], B_const], writes=[B_sdh[t]])
                S.op("dve", _recip(rstdh[t][:, 0:1], sdh[t][:, 0:1]), reads=[B_sdh[t]], writes=[B_rstdh[t]])

            def Zb(t):
                zb = [B_bk[2 * t], B_bk[2 * t + 1]]
                S.op("dve", _stt(TB[t][:, :], PS[t][:, :], rstdh[t][:, 0:1], gpostmlp[:, :], ALU.mult, ALU.mult),
                     reads=zb + [B_rstdh[t], B_const], writes=[B_TB[t]])
                S.op("pool", _tt(TB[t][:, :], TB[t][:, :], xres[:, t, :], ALU.add),
                     reads=[B_TB[t], B_x[t]], writes=[B_TB[t]])
                S.dma("pool", _dma(out[r0 + t * 128:r0 + (t + 1) * 128, :], TB[t][:, :]), f"ost{t}",
                      reads=[B_TB[t]], writes=[B_out[t]])

            for hh in range(2):
                for blk in range(4):
                    so = ring_load(22 + hh * 4 + blk)
                    wmo_ = slab3(so)
                    last_slab = (hh == 1 and blk == 3)
                    order = ([(cc, t) for t in range(4) for cc in range(8)] if last_slab
                             else [(cc, t) for cc in range(8) for t in range(4)])
                    for (cc, t) in order:
                        f = blk * 8 + cc
                        S.op("pe", _mm(bank(2 * t + hh), R[:, f, t * 128:(t + 1) * 128], wmo_[:, cc, :], f == 0, f == 31),
                             reads=[B_R[f], B_ring[so]], writes=[B_bk[2 * t + hh]])
                        if last_slab and cc == 7:
                            Za(t)
                            if t >= 1:
                                Zb(t - 1)
            Zb(3)

        S.wait_all("sp", B_out + [B_dbg])

        eng_sems = {}
        for e in ENGS:
            for ep in range(S.n_epochs):
                eng_sems[(e, ep)] = st.enter_context(nc.semaphore(f"s_{e}_{ep}"))
        dma_sems = {k: st.enter_context(nc.semaphore(f"d_{k}")) for k in S.dma_keys()}
        block = st.enter_context(nc.Block())
        stats = S.emit(block, eng_sems, dma_sems)
        build_program.last_stats = stats
    return nc


def _rope_tables(real_idx):
    quarter = 32
    inv_freq = (10000.0 ** (-np.arange(quarter, dtype=np.float32) / np.float32(quarter))).astype(np.float32)
    idx = np.asarray(real_idx)
    valid = idx >= 0
    rows = np.where(valid, idx // 64, 0).astype(np.float32)
    cols = np.where(valid, idx % 64, 0).astype(np.float32)
    ang_r = (rows[:, None] * inv_freq[None, :]).astype(np.float32)
    ang_c = (cols[:, None] * inv_freq[None, :]).astype(np.float32)
    cr, sr, cc, sc = np.cos(ang_r), np.sin(ang_r), np.cos(ang_c), np.sin(ang_c)
    C = np.concatenate([cr, cr, cc, cc], axis=1)
    Sn = np.concatenate([-sr, sr, -sc, sc], axis=1)
    return np.concatenate([C, Sn], axis=1).astype(np.float32)


def _swap32(g):
    g4 = np.asarray(g, np.float32).reshape(2, 2, 32)
    return np.ascontiguousarray(g4[:, ::-1, :]).reshape(128)


def make_in_maps(x, meta_tokens, pre_mix_g, q_norm_g, k_norm_g, w_in, w_attn_br, w_pool_grp, pool_scale,
                 w_pool_br, w_out, post_mix_g, pre_mlp_g, w_mlp_in, w_mlp_out, post_mlp_g):
    f = np.float32
    x = np.asarray(x, f)
    meta = np.asarray(meta_tokens, f)
    shared = {
        "w_in": np.ascontiguousarray(np.asarray(w_in, f)[0]),
        "w_attn_br": np.ascontiguousarray(np.asarray(w_attn_br, f)[0]),
        "w_pool_grp": np.ascontiguousarray(np.asarray(w_pool_grp, f)[0]),
        "w_pool_br": np.ascontiguousarray(np.asarray(w_pool_br, f)[0]),
        "w_out": np.ascontiguousarray(np.asarray(w_out, f)[0]),
        "w_mlp_in": np.ascontiguousarray(np.asarray(w_mlp_in, f)[0]),
        "w_mlp_out": np.ascontiguousarray(np.asarray(w_mlp_out, f)[0]),
        "g_premix_c": np.ascontiguousarray(np.asarray(pre_mix_g, f)[0].reshape(8, 128).T),
        "g_premlp_c": np.ascontiguousarray(np.asarray(pre_mlp_g, f)[0].reshape(8, 128).T),
        "g_pscale_c": np.ascontiguousarray(np.asarray(pool_scale, f)[0].reshape(4, 128).T),
        "g_postmix": np.ascontiguousarray(np.asarray(post_mix_g, f)[0]),
        "g_postmlp": np.ascontiguousarray(np.asarray(post_mlp_g, f)[0]),
        "g_q": np.ascontiguousarray(np.asarray(q_norm_g, f)[0]),
        "g_qs": _swap32(np.asarray(q_norm_g, f)[0]),
        "g_k": np.ascontiguousarray(np.asarray(k_norm_g, f)[0]),
        "g_ks": _swap32(np.asarray(k_norm_g, f)[0]),
        "ident": np.eye(128, dtype=f).astype(ml_dtypes.bfloat16),
    }
    kidx = np.concatenate([np.full(128, -1), np.arange(SEQ)])
    ropeK = _rope_tables(kidx).reshape(NKT, 128, 256)
    shared["ropeK"] = ropeK
    in_maps = []
    for core in range(8):
        b, hf = core // 2, core % 2
        xkv = np.zeros((NKT * 128, D), f)
        xkv[0:N_META] = meta
        xkv[128:] = x[b]
        q0 = hf * NQ
        xq = np.zeros((NQ + 16, D), f)
        xq[8:8 + NQ] = x[b, q0:q0 + NQ]
        if hf == 0:
            xq[0:8] = meta[8:16]
            xq[8 + NQ:] = x[b, NQ:NQ + 8]
        else:
            xq[0:8] = x[b, q0 - 8:q0]
        ropeQ = _rope_tables(np.arange(q0, q0 + NQ)).reshape(NQ // 128, 128, 256)
        inv = np.zeros((NG, 4, 16), f)
        for G in range(NG):
            for g in range(4):
                w_ = 2 << g
                treal = q0 + G * GT + GT - 16 + np.arange(16)
                tpos = treal + N_META
                lo = np.clip(tpos - w_ // 2, 0, LTOT)
                hi = np.clip(tpos + w_ - w_ // 2, 0, LTOT)
                inv[G, g] = 1.0 / (hi - lo).astype(f)
        m = dict(shared)
        m["xkv"] = xkv
        m["xq"] = xq
        m["ropeQ"] = ropeQ
        m["invtail"] = inv.reshape(-1)
        in_maps.append(m)
    return in_maps


_NC_CACHE = {}


def kernel(**inputs):
    in_maps = make_in_maps(**inputs)
    if "nc" not in _NC_CACHE:
        _NC_CACHE["nc"] = build_program()
    nc = _NC_CACHE["nc"]
    res = run_bass_kernel_spmd(nc, in_maps, core_ids=list(range(8)))
    outp = np.empty((BATCH, SEQ, D), np.float32)
    for core in range(8):
        b, hf = core // 2, core % 2
        outp[b, hf * NQ:(hf + 1) * NQ] = np.asarray(res.results[core]["out"], np.float32)
    return outp
```

```python
import math
from contextlib import ExitStack

import numpy as np
import ml_dtypes
import concourse.bass as bass
import concourse.mybir as mybir
from concourse.bass_utils import run_bass_kernel_spmd

F32 = mybir.dt.float32
BF16 = mybir.dt.bfloat16
AF = mybir.ActivationFunctionType
ALU = mybir.AluOpType
AX = mybir.AxisListType

D = 1024
SEQ = 8192
BATCH = 4
N_META = 16
LTOT = SEQ + N_META
NQ = 4096
NG = 8
GT = 512
NKT = 65
EPS = 1e-6
NSLAB = 30
NSLOT = 4

ENGS = ("pe", "act", "dve", "pool", "sp")


class Buf:
    __slots__ = ("name", "last_w", "readers")

    def __init__(self, name):
        self.name = name
        self.last_w = None
        self.readers = {}


class Ins:
    __slots__ = ("eng", "fn", "deps", "needs_inc", "ticket", "dma_sem", "dma_val", "epoch")

    def __init__(self, eng, fn, epoch):
        self.eng = eng
        self.fn = fn
        self.deps = []
        self.needs_inc = False
        self.ticket = None
        self.dma_sem = None
        self.dma_val = None
        self.epoch = epoch


class Sched:
    def __init__(self):
        self.streams = {e: [] for e in ENGS}
        self.dma_counts = {}
        self.epoch = 0
        self.n_epochs = 1

    def next_epoch(self):
        self.epoch += 1
        self.n_epochs = self.epoch + 1

    def _collect(self, reads, writes):
        deps = []
        for b in reads:
            if b.last_w is not None:
                deps.append(b.last_w)
        for b in writes:
            if b.last_w is not None:
                deps.append(b.last_w)
            deps.extend(b.readers.values())
        return [("d", d[1], self.dma_counts[d[1]]) if d[0] == "d" else d for d in deps]

    def _finish(self, entry, key, reads, writes):
        for b in reads:
            b.readers[key] = entry
        for b in writes:
            b.last_w = entry
            b.readers = {}

    def op(self, eng, fn, reads=(), writes=()):
        ins = Ins(eng, fn, self.epoch)
        for d in self._collect(reads, writes):
            if d[0] == "c" and d[1].eng == eng and eng == "pe":
                continue
            ins.deps.append(d)
        self.streams[eng].append(ins)
        self._finish(("c", ins), eng, reads, writes)
        return ins

    def dma(self, queue, fn, semkey, reads=(), writes=()):
        ins = Ins(queue, fn, self.epoch)
        ins.deps = list(self._collect(reads, writes))
        c = self.dma_counts.get(semkey, 0) + 16
        self.dma_counts[semkey] = c
        ins.dma_sem = semkey
        ins.dma_val = c
        self.streams[queue].append(ins)
        self._finish(("d", semkey, c), "dma:" + semkey, reads, writes)
        return ins

    def wait_all(self, eng, bufs):
        ins = Ins(eng, None, self.epoch)
        ins.deps = list(self._collect((), bufs))
        self.streams[eng].append(ins)
        return ins

    def dma_keys(self):
        return list(self.dma_counts.keys())

    def emit(self, block, eng_sems, dma_sems):
        for e in ENGS:
            for ins in self.streams[e]:
                for d in ins.deps:
                    if d[0] == "c":
                        d[1].needs_inc = True
        for e in ENGS:
            t = {}
            for ins in self.streams[e]:
                if ins.needs_inc:
                    t[ins.epoch] = t.get(ins.epoch, 0) + 1
                    ins.ticket = t[ins.epoch]
        stats = {}

        def run(e, engobj):
            waited = {}
            nw = 0
            for ins in self.streams[e]:
                need = {}
                for d in ins.deps:
                    if d[0] == "c":
                        k, v = ("e", d[1].eng, d[1].epoch), d[1].ticket
                    else:
                        k, v = ("d", d[1]), d[2]
                    if waited.get(k, 0) >= v:
                        continue
                    if need.get(k, 0) < v:
                        need[k] = v
                for k, v in need.items():
                    sem = eng_sems[(k[1], k[2])] if k[0] == "e" else dma_sems[k[1]]
                    engobj.wait_ge(sem, v)
                    waited[k] = v
                    nw += 1
                if ins.fn is None:
                    continue
                bi = ins.fn(engobj)
                if ins.dma_sem is not None:
                    bi.then_inc(dma_sems[ins.dma_sem], 16)
                elif ins.needs_inc:
                    bi.then_inc(eng_sems[(e, ins.epoch)], 1)
            stats[e] = (len(self.streams[e]), nw)

        @block.tensor
        def _(eng):
            run("pe", eng)

        @block.scalar
        def _(eng):
            run("act", eng)

        @block.vector
        def _(eng):
            run("dve", eng)

        @block.gpsimd
        def _(eng):
            run("pool", eng)

        @block.sync
        def _(eng):
            run("sp", eng)

        return stats


def _mm(out, lhsT, rhs, start, stop):
    return lambda e: e.matmul(out, lhsT, rhs, start=start, stop=stop)


def _tr(out, in_, idn):
    return lambda e: e.transpose(out, in_, idn)


def _act(out, in_, func, **kw):
    return lambda e: e.activation(out=out, in_=in_, func=func, **kw)


def _tt(out, in0, in1, op):
    return lambda e: e.tensor_tensor(out=out, in0=in0, in1=in1, op=op)


def _ts(out, in0, s1, op0, s2=None, op1=None):
    if op1 is None:
        return lambda e: e.tensor_scalar(out=out, in0=in0, scalar1=s1, scalar2=None, op0=op0)
    return lambda e: e.tensor_scalar(out=out, in0=in0, scalar1=s1, scalar2=s2, op0=op0, op1=op1)


def _stt(out, in0, scalar, in1, op0, op1):
    return lambda e: e.scalar_tensor_tensor(out=out, in0=in0, scalar=scalar, in1=in1, op0=op0, op1=op1)


def _recip(out, in_):
    return lambda e: e.reciprocal(out=out, in_=in_)


def _red(out, in_):
    return lambda e: e.tensor_reduce(out=out, in_=in_, axis=AX.X, op=ALU.add)


def _dma(out, in_):
    return lambda e: e.dma_start(out=out, in_=in_)


def _copy(out, in_):
    return lambda e: e.tensor_copy(out=out, in_=in_)


def _memset(ap, v):
    return lambda e: e.memset(ap, v)


def build_program(dbg=False, n_groups=NG, n_kt=NKT):
    nc = bass.Bass("TRN2", target_bir_lowering=False)

    def din(name, shape, dt=F32):
        return nc.dram_tensor(name, list(shape), dt, kind="ExternalInput").ap()

    xkv = din("xkv", [NKT * 128, D])
    xq = din("xq", [NQ + 16, D])
    w_in = din("w_in", [D, 4096])
    w_attn = din("w_attn_br", [D, D])
    w_grp = din("w_pool_grp", [4, 128, 128])
    w_pbr = din("w_pool_br", [512, D])
    w_out = din("w_out", [D, D])
    w_mi = din("w_mlp_in", [D, 4096])
    w_mo = din("w_mlp_out", [4096, D])
    g_premix_c = din("g_premix_c", [128, 8])
    g_premlp_c = din("g_premlp_c", [128, 8])
    g_pscale_c = din("g_pscale_c", [128, 4])
    g_postmix = din("g_postmix", [D])
    g_postmlp = din("g_postmlp", [D])
    g_q = din("g_q", [128])
    g_qs = din("g_qs", [128])
    g_k = din("g_k", [128])
    g_ks = din("g_ks", [128])
    ropeK = din("ropeK", [NKT, 128, 256])
    ropeQ = din("ropeQ", [NQ // 128, 128, 256])
    invtail = din("invtail", [NG * 4 * 16])
    ident_d = din("ident", [128, 128], BF16)
    out = nc.dram_tensor("out", [NQ, D], F32, kind="ExternalOutput").ap()
    wsl = nc.dram_tensor("wsl", [NSLAB, 128, 4096], BF16, kind="Internal").ap()
    wgrp_d = nc.dram_tensor("wgrp_b", [128, 4, 128], BF16, kind="Internal").ap()
    if dbg:
        dbg_kt = nc.dram_tensor("dbg_kt", [128, 2, LTOT], BF16, kind="ExternalOutput").ap()
        dbg_v = nc.dram_tensor("dbg_v", [128, NKT, 256], BF16, kind="ExternalOutput").ap()
        dbg_R = {k: nc.dram_tensor("dbg_" + k, [128, 32, 512], BF16, kind="ExternalOutput").ap()
                 for k in ("b3", "b4", "b5", "b6")}
        dbg_hT = nc.dram_tensor("dbg_hT", [128, 8, 528], BF16, kind="ExternalOutput").ap()
        dbg_x1 = nc.dram_tensor("dbg_x1", [128, 4, D], F32, kind="ExternalOutput").ap()

    def dump_R(key):
        if dbg:
            S.dma("sp", _dma(dbg_R[key], R[:]), "dbg", reads=B_R, writes=[B_dbg])

    S = Sched()
    with ExitStack() as st:
        def sb(name, shape, dt):
            return st.enter_context(nc.sbuf_tensor(name, list(shape), dt))

        KT = sb("KT", [128, 2, LTOT], BF16)
        V = sb("V", [128, NKT, 256], BF16)
        R = sb("R", [128, 32, 512], BF16)
        xres = sb("xres", [128, 4, D], F32)
        xhalo = sb("xhalo", [16, D], F32)
        hT = sb("hT", [128, 8, 528], BF16)
        TB = [sb(f"TB{i}", [128, D], F32) for i in range(5)]
        hbf = [sb(f"hbf{i}", [128, D], BF16) for i in range(2)]
        qrot = [sb(f"qrot{i}", [128, D], BF16) for i in range(2)]
        pooled = [sb(f"pooled{i}", [128, 512], BF16) for i in range(4)]
        gpostmix = sb("gpostmix", [128, D], F32)
        gpostmlp = sb("gpostmlp", [128, D], F32)
        gcolmix = sb("gcolmix", [128, 8], F32)
        gcolmlp = sb("gcolmlp", [128, 8], F32)
        pscale = sb("pscale", [128, 4], F32)
        gq = sb("gq", [128, 128], F32)
        gqs = sb("gqs", [128, 128], F32)
        gk = sb("gk", [128, 128], F32)
        gks = sb("gks", [128, 128], F32)
        invt = sb("invt", [128, 64], F32)
        ident = sb("ident_sb", [128, 128], BF16)
        ones = sb("ones", [128, 128], BF16)
        sel32 = sb("sel32", [128, 128], F32)
        epsT = sb("epsT", [128, 1], F32)
        gmx = sb("gmx", [128, 2], F32)
        negc = sb("negc", [128, 1], F32)
        rtab = [sb(f"rtab{i}", [128, 256], F32) for i in range(2)]
        cs = [sb(f"cs{i}", [128, 256], F32) for i in range(2)]
        ss = [sb(f"ss{i}", [128, 8], F32) for i in range(4)]
        sd = [sb(f"sd{i}", [128, 8], F32) for i in range(4)]
        rstd = [sb(f"rstd{i}", [128, 8], F32) for i in range(4)]
        ssh = [sb(f"ssh{i}", [128, 8], F32) for i in range(4)]
        sdh = [sb(f"sdh{i}", [128, 8], F32) for i in range(4)]
        rstdh = [sb(f"rstdh{i}", [128, 8], F32) for i in range(4)]
        tail16 = sb("tail16", [128, 16], F32)
        wgrp = sb("wgrp", [128, 4, 128], BF16)
        ring = [sb(f"ring{i}", [128, 4096], BF16) for i in range(NSLOT)]
        PS = [st.enter_context(nc.psum_tensor(f"ps{i}", [128, 1024], F32)) for i in range(4)]

        def bank(b):
            return PS[b // 2][:, (b % 2) * 512:(b % 2) * 512 + 512]

        def bankb(b):
            return bank(b).bitcast(BF16)

        B_KT = [Buf(f"KT{i}") for i in range(NKT)]
        B_V = [Buf(f"V{i}") for i in range(NKT)]
        B_R = [Buf(f"R{i}") for i in range(32)]
        B_x = [Buf(f"x{i}") for i in range(4)]
        B_xh = Buf("xhalo")
        B_hT = [Buf(f"hT{i}") for i in range(4)]
        B_hTh = Buf("hTh")
        B_TB = [Buf(f"TB{i}") for i in range(5)]
        B_hbf = [Buf(f"hbf{i}") for i in range(2)]
        B_qrot = [Buf(f"qrot{i}") for i in range(2)]
        B_pooled = [Buf(f"pooled{i}") for i in range(4)]
        B_const = Buf("const")
        B_negc = Buf("negc")
        B_rtab = [Buf(f"rtab{i}") for i in range(2)]
        B_cs = [Buf(f"cs{i}") for i in range(2)]
        B_ss = [Buf(f"ss{i}") for i in range(4)]
        B_sd = [Buf(f"sd{i}") for i in range(4)]
        B_rstd = [Buf(f"rstd{i}") for i in range(4)]
        B_ssh = [Buf(f"ssh{i}") for i in range(4)]
        B_sdh = [Buf(f"sdh{i}") for i in range(4)]
        B_rstdh = [Buf(f"rstdh{i}") for i in range(4)]
        B_tail = Buf("tail16")
        B_invt = Buf("invt")
        B_wgrp = Buf("wgrp")
        B_wgrpd = Buf("wgrpd")
        B_ring = [Buf(f"ring{i}") for i in range(NSLOT)]
        B_ws = [Buf(f"ws{i}") for i in range(NSLAB)]
        B_bk = [Buf(f"bank{i}") for i in range(8)]
        B_out = [Buf(f"out{i}") for i in range(4)]
        B_dbg = Buf("dbg")

        B_cl = []

        def cload(dst, src):
            b_ = Buf("c%d" % len(B_cl))
            B_cl.append(b_)
            S.dma("sp", _dma(dst, src), "const", writes=[b_])

        cload(ident[:], ident_d)
        cload(gcolmix[:], g_premix_c)
        cload(gcolmlp[:], g_premlp_c)
        cload(pscale[:], g_pscale_c)
        cload(gk[:], g_k.partition_broadcast(128))
        cload(gks[:], g_ks.partition_broadcast(128))
        cload(gq[:], g_q.partition_broadcast(128))
        cload(gqs[:], g_qs.partition_broadcast(128))
        cload(gpostmix[:], g_postmix.partition_broadcast(128))
        cload(gpostmlp[:], g_postmlp.partition_broadcast(128))
        S.op("dve", _memset(ones[:], 1.0), writes=[B_const])
        S.op("dve", _memset(sel32[:], 1.0), writes=[B_const])
        S.op("dve", _memset(epsT[:], EPS), reads=B_cl, writes=[B_const])
        S.op("dve", lambda e: e.tensor_reduce(out=gmx[:, 0:1], in_=gq[:, :], axis=AX.X, op=ALU.max, apply_absolute_value=True),
             reads=[B_const], writes=[B_negc])
        S.op("dve", lambda e: e.tensor_reduce(out=gmx[:, 1:2], in_=gk[:, :], axis=AX.X, op=ALU.max, apply_absolute_value=True),
             reads=[B_const, B_negc], writes=[B_negc])
        S.op("dve", _tt(negc[:, :], gmx[:, 0:1], gmx[:, 1:2], ALU.mult), reads=[B_negc], writes=[B_negc])
        S.op("dve", _ts(negc[:, :], negc[:, :], -math.sqrt(128.0), ALU.mult), reads=[B_negc], writes=[B_negc])

        def cast(slab, pieces, key):
            for (src, c0, nch, ncols) in pieces:
                dst = wsl[slab][:, c0:c0 + nch * ncols].rearrange("p (c n) -> p c n", c=nch)
                srcv = src.rearrange("(c p) n -> p c n", p=128)
                S.dma("pool", _dma(dst, srcv), key, writes=[B_ws[slab]])

        cast(0, [(w_in[:, 1024:1536], 0, 8, 512)], "cast0")
        def early_casts():
            cast(1, [(w_in[:, 0:512], 0, 8, 512)], "castA")
            cast(2, [(w_in[:, 512:1024], 0, 8, 512)], "castA")
            cast(3, [(w_in[:, 1536:2048], 0, 8, 512)], "castA")
            S.dma("pool", _dma(wgrp_d, w_grp.rearrange("g c d -> c g d")), "castA", writes=[B_wgrpd])

        def deferred_casts():
            for j in range(8):
                cast(4 + j, [
                    (w_attn[:, j * 128:(j + 1) * 128], 0, 8, 128),
                    (w_in[:, 2048 + j * 128:2048 + (j + 1) * 128], 1024, 8, 128),
                    (w_in[:, 3072 + j * 128:3072 + (j + 1) * 128], 2048, 8, 128),
                    (w_pbr[:, j * 128:(j + 1) * 128], 3072, 4, 128),
                ], "castC")
            for hh in range(2):
                cast(12 + hh, [(w_out[:, hh * 512:(hh + 1) * 512], 0, 8, 512)], "castC")
            for s_ in range(8):
                cast(14 + s_, [(w_mi[:, s_ * 512:(s_ + 1) * 512], 0, 8, 512)], "castB")
            for hh in range(2):
                for blk in range(4):
                    cast(22 + hh * 4 + blk,
                         [(w_mo[blk * 1024:(blk + 1) * 1024, hh * 512:(hh + 1) * 512], 0, 8, 512)], "castB")

        ring_ctr = [0]

        def ring_load(slab, ncols=4096):
            slot = ring_ctr[0] % NSLOT
            ring_ctr[0] += 1
            S.dma("sp", _dma(ring[slot][:, 0:ncols], wsl[slab][:, 0:ncols]), f"ring{slot}",
                  reads=[B_ws[slab]], writes=[B_ring[slot]])
            return slot

        def slab3(slot):
            return ring[slot][:].rearrange("p (c n) -> p c n", c=8)

        class Ctx:
            pass

        def mk_ctx(i, tA, bA, tU, bU, rt, brt, cs_, bcs, rot, brot, hb, bhb):
            c = Ctx()
            c.ss, c.b_ss = ss[i], B_ss[i]
            c.sd, c.b_sd = sd[i], B_sd[i]
            c.rstd, c.b_rstd = rstd[i], B_rstd[i]
            c.ssh, c.b_ssh = ssh[i], B_ssh[i]
            c.sdh, c.b_sdh = sdh[i], B_sdh[i]
            c.rstdh, c.b_rstdh = rstdh[i], B_rstdh[i]
            c.tA, c.b_tA, c.tU, c.b_tU = tA, bA, tU, bU
            c.rtab, c.b_rtab, c.cs, c.b_cs = rt, brt, cs_, bcs
            c.rot, c.b_rot, c.hbf, c.b_hbf = rot, brot, hb, bhb
            return c

        ctxB = [mk_ctx(i, TB[i], [B_TB[i]], TB[2 + i], [B_TB[2 + i]], rtab[i], [B_rtab[i]], cs[i], [B_cs[i]],
                       qrot[i], [B_qrot[i]], hbf[i], [B_hbf[i]]) for i in range(2)]
        ctxA = []
        for i in range(4):
            ctxA.append(mk_ctx(
                i,
                R[:, i, :].bitcast(F32), [B_R[i]],
                R[:, 4 + i, :].bitcast(F32), [B_R[4 + i]],
                R[:, 12 + i, :].bitcast(F32), [B_R[12 + i]],
                R[:, 16 + i, :].bitcast(F32), [B_R[16 + i]],
                R[:, 8 + i, :], [B_R[8 + i]],
                R[:, 20 + 2 * i:22 + 2 * i, :].rearrange("p a n -> p (a n)"), [B_R[20 + 2 * i], B_R[21 + 2 * i]]))

        def emit_h(xap, xbuf, rows, cx, bT, gcol, dsts):
            S.op("act", _act(cx.hbf[0:rows, :], xap, AF.Square, accum_out=cx.ss[0:rows, 0:1]),
                 reads=[xbuf], writes=[cx.b_ss] + cx.b_hbf)
            S.op("act", _act(cx.sd[0:rows, 0:1], cx.ss[0:rows, 0:1], AF.Sqrt, bias=epsT[0:rows, :], scale=1.0 / D),
                 reads=[cx.b_ss, B_const], writes=[cx.b_sd])
            S.op("dve", _recip(cx.rstd[0:rows, 0:1], cx.sd[0:rows, 0:1]), reads=[cx.b_sd], writes=[cx.b_rstd])
            S.op("act", _act(cx.hbf[0:rows, :], xap, AF.Copy, scale=cx.rstd[0:rows, 0:1]),
                 reads=[xbuf, cx.b_rstd], writes=cx.b_hbf)
            pb = bankb(bT)
            for c in range(8):
                S.op("pe", _tr(pb[:, c * 128:c * 128 + rows], cx.hbf[0:rows, c * 128:(c + 1) * 128], ident[0:rows, 0:rows]),
                     reads=cx.b_hbf + [B_const], writes=[B_bk[bT]])
            pb3 = pb.rearrange("p (c n) -> p c n", c=8)
            for (hc0, pc0, ncol, dbuf) in dsts:
                S.op("dve", _tt(hT[:, :, hc0:hc0 + ncol], pb3[:, :, pc0:pc0 + ncol],
                                gcol[:].unsqueeze(2).broadcast_to([128, 8, ncol]), ALU.mult),
                     reads=[B_bk[bT], B_const], writes=[dbuf])

        def nr_square(X, xbank_bufs, H, cx):
            n = H * 128
            S.op("act", _act(cx.tA[:, 0:n], X, AF.Square), reads=xbank_bufs, writes=cx.b_tA)

        def nr_rest(X, xbank_bufs, H, cx, gt, gst):
            n = H * 128
            tA, tU = cx.tA, cx.tU
            S.op("dve", _red(cx.ssh[:, 0:H], tA[:, 0:n].rearrange("p (h d) -> p h d", h=H)),
                 reads=cx.b_tA, writes=[cx.b_ssh])
            S.op("act", _act(cx.sdh[:, 0:H], cx.ssh[:, 0:H], AF.Sqrt, bias=epsT[:, :], scale=1.0 / 128),
                 reads=[cx.b_ssh, B_const], writes=[cx.b_sdh])
            S.op("dve", _recip(cx.rstdh[:, 0:H], cx.sdh[:, 0:H]), reads=[cx.b_sdh], writes=[cx.b_rstdh])
            S.op("pool", _tt(cx.cs[:, 0:128], cx.rtab[:, 0:128], gt[:], ALU.mult),
                 reads=cx.b_rtab + [B_const], writes=cx.b_cs)
            S.op("pool", _tt(cx.cs[:, 128:256], cx.rtab[:, 128:256], gst[:], ALU.mult),
                 reads=cx.b_rtab + [B_const], writes=cx.b_cs)
            X5 = X.rearrange("p (h a b j) -> p h a b j", h=H, a=2, b=2, j=32)
            T5 = tA[:, 0:n].rearrange("p (h a b j) -> p h a b j", h=H, a=2, b=2, j=32)
            Sg = cx.cs[:, 128:256].rearrange("p (a b j) -> p a b j", a=2, b=2, j=32)
            for bsel in range(2):
                S.op("dve", _tt(T5[:, :, :, bsel, :], X5[:, :, :, 1 - bsel, :],
                                Sg[:, :, bsel, :].unsqueeze(1).broadcast_to([128, H, 2, 32]), ALU.mult),
                     reads=xbank_bufs + cx.b_cs, writes=cx.b_tA)
            S.op("dve", _tt(tU[:, 0:n].rearrange("p (h d) -> p h d", h=H), X.rearrange("p (h d) -> p h d", h=H),
                            cx.cs[:, 0:128].unsqueeze(1).broadcast_to([128, H, 128]), ALU.mult),
                 reads=xbank_bufs + cx.b_cs, writes=cx.b_tU)
            S.op("pool", _tt(tU[:, 0:n], tU[:, 0:n], tA[:, 0:n], ALU.add),
                 reads=cx.b_tA + cx.b_tU, writes=cx.b_tU)
            S.op("dve", _tt(cx.rot[:, 0:n].rearrange("p (h d) -> p h d", h=H), tU[:, 0:n].rearrange("p (h d) -> p h d", h=H),
                            cx.rstdh[:, 0:H].unsqueeze(2).broadcast_to([128, H, 128]), ALU.mult),
                 reads=cx.b_tU + [cx.b_rstdh], writes=cx.b_rot)

        kvslot = ring_load(0)
        wkv = slab3(kvslot)

        def a_ctx(kt):
            return ctxA[kt % 4]

        def sqbuf(kt):
            return R[:, 28 + kt % 4, :].bitcast(F32), [B_R[28 + kt % 4]]

        def S1(kt):
            q = kt % 4
            cx = a_ctx(kt)
            xap, xbuf = xres[:, q, :], B_x[q]
            S.dma("sp", _dma(xap, xkv[kt * 128:(kt + 1) * 128, :]), f"xa{q}", writes=[xbuf])
            S.dma("sp", _dma(cx.rtab, ropeK[kt]), f"rt{q}", writes=cx.b_rtab)
            S.op("act", _act(cx.hbf[:, :], xap, AF.Square, accum_out=cx.ss[:, 0:1]), reads=[xbuf],
                 writes=[cx.b_ss] + cx.b_hbf)
            S.op("act", _act(cx.sd[:, 0:1], cx.ss[:, 0:1], AF.Sqrt, bias=epsT[:, :], scale=1.0 / D),
                 reads=[cx.b_ss, B_const], writes=[cx.b_sd])
            S.op("dve", _recip(cx.rstd[:, 0:1], cx.sd[:, 0:1]), reads=[cx.b_sd], writes=[cx.b_rstd])

        def S1b(kt):
            q = kt % 4
            cx = a_ctx(kt)
            xap, xbuf = xres[:, q, :], B_x[q]
            S.op("act", _act(cx.hbf[:, :], xap, AF.Copy, scale=cx.rstd[:, 0:1]),
                 reads=[xbuf, cx.b_rstd], writes=cx.b_hbf)

        def S2(kt):
            q = kt % 4
            cx = a_ctx(kt)
            bT, bkv = kt % 3, 3 + kt % 3
            pb = bankb(bT)
            for c in range(8):
                S.op("pe", _tr(pb[:, c * 128:(c + 1) * 128], cx.hbf[:, c * 128:(c + 1) * 128], ident[:, :]),
                     reads=cx.b_hbf + [B_const], writes=[B_bk[bT]])
            pb3 = pb.rearrange("p (c n) -> p c n", c=8)
            S.op("dve", _tt(hT[:, :, 8 + q * 128:8 + (q + 1) * 128], pb3[:, :, :],
                            gcolmix[:].unsqueeze(2).broadcast_to([128, 8, 128]), ALU.mult),
                 reads=[B_bk[bT], B_const], writes=[B_hT[q]])
            for c in range(8):
                S.op("pe", _mm(bank(bkv), hT[:, c, 8 + q * 128:8 + (q + 1) * 128], wkv[:, c, :], c == 0, c == 7),
                     reads=[B_hT[q], B_ring[kvslot]], writes=[B_bk[bkv]])

        def S3(kt):
            cx = a_ctx(kt)
            bkv = 3 + kt % 3
            X = bank(bkv)[:, 0:256]
            xb_ = [B_bk[bkv]]
            sq, bsq = sqbuf(kt)
            S.op("dve", _copy(V[:, kt, :], bank(bkv)[:, 256:512]), reads=xb_, writes=[B_V[kt]])
            S.op("act", _act(sq[:, :], X, AF.Square), reads=xb_, writes=bsq)
            S.op("pool", _tt(cx.cs[:, 0:128], cx.rtab[:, 0:128], gk[:], ALU.mult),
                 reads=cx.b_rtab + [B_const], writes=cx.b_cs)
            S.op("pool", _tt(cx.cs[:, 128:256], cx.rtab[:, 128:256], gks[:], ALU.mult),
                 reads=cx.b_rtab + [B_const], writes=cx.b_cs)
            S.op("dve", _red(cx.ssh[:, 0:2], sq[:, :].rearrange("p (h d) -> p h d", h=2)),
                 reads=bsq, writes=[cx.b_ssh])
            X5 = X.rearrange("p (h a b j) -> p h a b j", h=2, a=2, b=2, j=32)
            T5 = cx.tA[:, :].rearrange("p (h a b j) -> p h a b j", h=2, a=2, b=2, j=32)
            Sg = cx.cs[:, 128:256].rearrange("p (a b j) -> p a b j", a=2, b=2, j=32)
            for bsel in range(2):
                S.op("dve", _tt(T5[:, :, :, bsel, :], X5[:, :, :, 1 - bsel, :],
                                Sg[:, :, bsel, :].unsqueeze(1).broadcast_to([128, 2, 2, 32]), ALU.mult),
                     reads=xb_ + cx.b_cs, writes=cx.b_tA)
            S.op("dve", _tt(cx.tU[:, :].rearrange("p (h d) -> p h d", h=2), X.rearrange("p (h d) -> p h d", h=2),
                            cx.cs[:, 0:128].unsqueeze(1).broadcast_to([128, 2, 128]), ALU.mult),
                 reads=xb_ + cx.b_cs, writes=cx.b_tU)

        def S4(kt):
            cx = a_ctx(kt)
            S.op("act", _act(cx.sdh[:, 0:2], cx.ssh[:, 0:2], AF.Sqrt, bias=epsT[:, :], scale=1.0 / 128),
                 reads=[cx.b_ssh, B_const], writes=[cx.b_sdh])
            S.op("dve", _recip(cx.rstdh[:, 0:2], cx.sdh[:, 0:2]), reads=[cx.b_sdh], writes=[cx.b_rstdh])
            S.op("pool", _tt(cx.tU[:, :], cx.tU[:, :], cx.tA[:, :], ALU.add),
                 reads=cx.b_tA + cx.b_tU, writes=cx.b_tU)

        def S4b(kt):
            cx = a_ctx(kt)
            S.op("pool", _tt(cx.rot[:, 0:256].rearrange("p (h d) -> p h d", h=2), cx.tU[:, :].rearrange("p (h d) -> p h d", h=2),
                            cx.rstdh[:, 0:2].unsqueeze(2).broadcast_to([128, 2, 128]), ALU.mult),
                 reads=cx.b_tU + [cx.b_rstdh], writes=cx.b_rot)

        def S5(kt):
            cx = a_ctx(kt)
            bkt = 6 + kt % 2
            pb = bankb(bkt)
            for h in range(2):
                S.op("pe", _tr(pb[:, h * 128:(h + 1) * 128], cx.rot[:, h * 128:(h + 1) * 128], ident[:]),
                     reads=cx.b_rot + [B_const], writes=[B_bk[bkt]])
            ntok = 16 if kt == 0 else 128
            col0 = 0 if kt == 0 else 16 + (kt - 1) * 128
            S.op("dve", _copy(KT[:, :, col0:col0 + ntok], pb.rearrange("p (c n) -> p c n", c=8)[:, 0:2, 0:ntok]),
                 reads=[B_bk[bkt]], writes=[B_KT[kt]])

        stages = [S1, S1b, S2, S3, S4, S4b, S5]
        for step in range(n_kt + len(stages) - 1):
            if step == min(8, n_kt - 1):
                early_casts()
            for si, fn in enumerate(stages):
                k_ = step - si
                if 0 <= k_ < n_kt:
                    fn(k_)

        if dbg:
            S.dma("sp", _dma(dbg_kt, KT[:]), "dbg", reads=B_KT, writes=[B_dbg])
            S.dma("sp", _dma(dbg_v, V[:]), "dbg", reads=B_V, writes=[B_dbg])

        scale_qk = 1.0 / math.sqrt(128.0)

        def modulo(stage_fns, n):
            for step in range(n + len(stage_fns) - 1):
                for si in reversed(range(len(stage_fns))):
                    t_ = step - si
                    if 0 <= t_ < n:
                        stage_fns[si](t_)

        for G in range(n_groups):
            S.next_epoch()
            r0 = G * GT
            sq0 = ring_load(1)
            sq1 = ring_load(2)
            wq = [slab3(sq0), slab3(sq1)]
            wqb = [B_ring[sq0], B_ring[sq1]]
            S.dma("sp", _dma(xhalo[0:8, :], xq[r0:r0 + 8, :]), "xlh", writes=[B_xh])
            S.dma("sp", _dma(xhalo[8:16, :], xq[8 + r0 + GT:8 + r0 + GT + 8, :]), "xlh", writes=[B_xh])
            for t in range(4):
                S.dma("sp", _dma(xres[:, t, :], xq[8 + r0 + t * 128:8 + r0 + (t + 1) * 128, :]), f"xl{t}",
                      writes=[B_x[t]])
            S.dma("sp", _dma(invt[:], invtail[G * 64:(G + 1) * 64].partition_broadcast(128)), "invt", writes=[B_invt])

            def Ha(t, gcol_unused=None):
                cx = ctxB[t % 2]
                xap, xbuf = xres[:, t, :], B_x[t]
                S.op("act", _act(cx.hbf[:, :], xap, AF.Square, accum_out=cx.ss[:, 0:1]),
                     reads=[xbuf], writes=[cx.b_ss] + cx.b_hbf)
                S.op("act", _act(cx.sd[:, 0:1], cx.ss[:, 0:1], AF.Sqrt, bias=epsT[:, :], scale=1.0 / D),
                     reads=[cx.b_ss, B_const], writes=[cx.b_sd])
                S.op("dve", _recip(cx.rstd[:, 0:1], cx.sd[:, 0:1]), reads=[cx.b_sd], writes=[cx.b_rstd])

            def Ha2(t):
                cx = ctxB[t % 2]
                xap, xbuf = xres[:, t, :], B_x[t]
                S.op("act", _act(cx.hbf[:, :], xap, AF.Copy, scale=cx.rstd[:, 0:1]),
                     reads=[xbuf, cx.b_rstd], writes=cx.b_hbf)

            def mk_Hb(gcol):
                def Hb(t):
                    cx = ctxB[t % 2]
                    bT = 6 + t % 2
                    pb = bankb(bT)
                    for c in range(8):
                        S.op("pe", _tr(pb[:, c * 128:(c + 1) * 128], cx.hbf[:, c * 128:(c + 1) * 128], ident[:, :]),
                             reads=cx.b_hbf + [B_const], writes=[B_bk[bT]])
                    S.op("dve", _tt(hT[:, :, 8 + t * 128:8 + (t + 1) * 128], pb.rearrange("p (c n) -> p c n", c=8),
                                    gcol[:].unsqueeze(2).broadcast_to([128, 8, 128]), ALU.mult),
                         reads=[B_bk[bT], B_const], writes=[B_hT[t]])
                return Hb

            def Hh():
                emit_h(xhalo[:, :], B_xh, 16, ctxB[0], 6, gcolmix, [(0, 0, 8, B_hTh), (520, 8, 8, B_hTh)])

            def Qa(t):
                par = t % 2
                S.dma("sp", _dma(rtab[par][:], ropeQ[G * 4 + t]), f"rq{par}", writes=[B_rtab[par]])
                for hh in range(2):
                    for c in range(8):
                        S.op("pe", _mm(bank(2 * par + hh), hT[:, c, 8 + t * 128:8 + (t + 1) * 128], wq[hh][:, c, :], c == 0, c == 7),
                             reads=[B_hT[t], wqb[hh]], writes=[B_bk[2 * par + hh]])
                S.op("act", _act(TB[4][:, :], PS[par][:, :], AF.Square),
                     reads=[B_bk[2 * par], B_bk[2 * par + 1]], writes=[B_TB[4]])

            def Qb(t):
                par = t % 2
                cx = ctxB[par]
                X = PS[par][:, :]
                xb_ = [B_bk[2 * par], B_bk[2 * par + 1]]
                S.op("dve", _red(cx.ssh[:, 0:8], TB[4][:, :].rearrange("p (h d) -> p h d", h=8)),
                     reads=[B_TB[4]], writes=[cx.b_ssh])
                S.op("pool", _tt(cx.cs[:, 0:128], cx.rtab[:, 0:128], gq[:], ALU.mult),
                     reads=cx.b_rtab + [B_const], writes=cx.b_cs)
                S.op("pool", _tt(cx.cs[:, 128:256], cx.rtab[:, 128:256], gqs[:], ALU.mult),
                     reads=cx.b_rtab + [B_const], writes=cx.b_cs)
                X5 = X.rearrange("p (h a b j) -> p h a b j", h=8, a=2, b=2, j=32)
                T5 = cx.tA[:, :].rearrange("p (h a b j) -> p h a b j", h=8, a=2, b=2, j=32)
                Sg = cx.cs[:, 128:256].rearrange("p (a b j) -> p a b j", a=2, b=2, j=32)
                for bsel in range(2):
                    S.op("dve", _tt(T5[:, :, :, bsel, :], X5[:, :, :, 1 - bsel, :],
                                    Sg[:, :, bsel, :].unsqueeze(1).broadcast_to([128, 8, 2, 32]), ALU.mult),
                         reads=xb_ + cx.b_cs, writes=cx.b_tA)
                S.op("dve", _tt(cx.tU[:, :].rearrange("p (h d) -> p h d", h=8), X.rearrange("p (h d) -> p h d", h=8),
                                cx.cs[:, 0:128].unsqueeze(1).broadcast_to([128, 8, 128]), ALU.mult),
                     reads=xb_ + cx.b_cs, writes=cx.b_tU)

            def Qc(t):
                cx = ctxB[t % 2]
                S.op("act", _act(cx.sdh[:, 0:8], cx.ssh[:, 0:8], AF.Sqrt, bias=epsT[:, :], scale=1.0 / 128),
                     reads=[cx.b_ssh, B_const], writes=[cx.b_sdh])
                S.op("dve", _recip(cx.rstdh[:, 0:8], cx.sdh[:, 0:8]), reads=[cx.b_sdh], writes=[cx.b_rstdh])
                S.op("pool", _tt(cx.tU[:, :], cx.tU[:, :], cx.tA[:, :], ALU.add),
                     reads=cx.b_tA + cx.b_tU, writes=cx.b_tU)

            def Qc2(t):
                cx = ctxB[t % 2]
                S.op("pool", _tt(cx.rot[:, :].rearrange("p (h d) -> p h d", h=8), cx.tU[:, :].rearrange("p (h d) -> p h d", h=8),
                                cx.rstdh[:, 0:8].unsqueeze(2).broadcast_to([128, 8, 128]), ALU.mult),
                     reads=cx.b_tU + [cx.b_rstdh], writes=cx.b_rot)

            def Qd(t):
                par = t % 2
                cx = ctxB[par]
                bq = 4 + par
                pb = bankb(bq)
                for h in range(8):
                    S.op("pe", _tr(pb[:, h * 128:(h + 1) * 128], cx.rot[:, h * 128:(h + 1) * 128], ident[:]),
                         reads=cx.b_rot + [B_const], writes=[B_bk[bq]])
                S.op("act", _act(R[:, 0:8, t * 128:(t + 1) * 128], pb.rearrange("p (c n) -> p c n", c=8), AF.Copy),
                     reads=[B_bk[bq]], writes=B_R[0:8])

            Hh()
            modulo([Ha, Ha2, mk_Hb(gcolmix), Qa, Qb, Qc, Qc2, Qd], 4)
            if G == 0:
                dump_R("b3")
                if dbg:
                    S.dma("sp", _dma(dbg_hT, hT[:]), "dbg", reads=B_hT + [B_hTh], writes=[B_dbg])
            sp_ = ring_load(3)
            wp = slab3(sp_)
            for g in range(4):
                par = g % 2
                w_ = 2 << g
                bm, bh = par, 2 + par
                for c in range(8):
                    S.op("pe", _mm(bank(bm), wp[:, c, g * 128:(g + 1) * 128], hT[:, c, 8:520], c == 0, c == 7),
                         reads=B_hT + [B_ring[sp_]], writes=[B_bk[bm]])
                for c in range(8):
                    S.op("pe", _mm(bank(bh)[:, 0:8], wp[:, c, g * 128:(g + 1) * 128], hT[:, c, 0:8], c == 0, c == 7),
                         reads=[B_hTh, B_ring[sp_]], writes=[B_bk[bh]])
                for c in range(8):
                    S.op("pe", _mm(bank(bh)[:, 8:16], wp[:, c, g * 128:(g + 1) * 128], hT[:, c, 520:528], c == 0, c == 7),
                         reads=[B_hTh, B_ring[sp_]], writes=[B_bk[bh]])
                pbuf = TB[par]
                S.op("act", _act(pbuf[:, 8:520], bank(bm), AF.Copy), reads=[B_bk[bm]], writes=[B_TB[par]])
                S.op("act", _act(pbuf[:, 0:8], bank(bh)[:, 0:8], AF.Copy), reads=[B_bk[bh]], writes=[B_TB[par]])
                S.op("act", _act(pbuf[:, 520:528], bank(bh)[:, 8:16], AF.Copy), reads=[B_bk[bh]], writes=[B_TB[par]])
                s2 = TB[2 + par]
                s4 = TB[4]
                S.op("pool", _tt(s2[:, 1:528], pbuf[:, 0:527], pbuf[:, 1:528], ALU.add),
                     reads=[B_TB[par]], writes=[B_TB[2 + par]])
                ssum, bsum = s2, B_TB[2 + par]
                if g >= 1:
                    S.op("pool", _tt(s4[:, 2:527], s2[:, 1:526], s2[:, 3:528], ALU.add),
                         reads=[B_TB[2 + par]], writes=[B_TB[4]])
                    ssum, bsum = s4, B_TB[4]
                if g >= 2:
                    S.op("pool", _tt(s2[:, 4:525], s4[:, 2:523], s4[:, 6:527], ALU.add),
                         reads=[B_TB[4]], writes=[B_TB[2 + par]])
                    ssum, bsum = s2, B_TB[2 + par]
                if g >= 3:
                    S.op("pool", _tt(s4[:, 8:521], s2[:, 4:517], s2[:, 12:525], ALU.add),
                         reads=[B_TB[2 + par]], writes=[B_TB[4]])
                    ssum, bsum = s4, B_TB[4]
                S.op("dve", _stt(pooled[g][:, :], ssum[:, 8:520], 1.0 / w_, pbuf[:, 8:520], ALU.mult, ALU.subtract),
                     reads=[bsum, B_TB[par]], writes=[B_pooled[g]])
                io = g * 16
                S.op("dve", _tt(tail16[:, :], ssum[:, 504:520], invt[:, io:io + 16], ALU.mult),
                     reads=[bsum, B_invt], writes=[B_tail])
                S.op("dve", _tt(pooled[g][:, 496:512], tail16[:, :], pbuf[:, 504:520], ALU.subtract),
                     reads=[B_tail, B_TB[par]], writes=[B_pooled[g]])
            if G == 0:
                deferred_casts()
            NP = 6
            PE_EVERY = 4
            npairs = (n_kt - 1) // 2
            units = [[1 + 2 * j, 2 + 2 * j] for j in range(npairs)] + [[0]]
            ntiles = sum(len(u_) for u_ in units)
            pending_fin = [None]
            uctr = [0]

            def fin_a(h):
                aD, bD = TB[h % 2], B_TB[h % 2]
                S.op("dve", _copy(TB[2 + h % 2][:, 0:512], bank(6)), reads=[B_bk[6]], writes=[B_TB[2 + h % 2]])
                S.op("dve", _tt(TB[4][:, 0:512], aD[:, 0:512], aD[:, 512:1024], ALU.add), reads=[bD], writes=[B_TB[4]])

            def fin_b(h):
                bo, btot = 6, 7
                S.op("pe", _mm(bank(btot), sel32[:, :], TB[4][:, 0:512], False, True),
                     reads=[B_TB[4], B_const], writes=[B_bk[btot]])
                S.op("dve", _recip(TB[4][:, 512:1024], bank(btot)), reads=[B_bk[btot]], writes=[B_TB[4]])
                S.op("dve", _tt(R[:, 8 + h, :], TB[2 + h % 2][:, 0:512], TB[4][:, 512:1024], ALU.mult),
                     reads=[B_TB[2 + h % 2], B_TB[4]], writes=[B_R[8 + h]])

            for h in range(8):
                kv = h // 4
                bo, btot = 6, 7
                aD, bD = TB[h % 2], B_TB[h % 2]
                npe = [0]
                tc = [0]
                ninit = {"dve": 0, "pool": 0}

                def qk(uu, unit):
                    for idx, kt in enumerate(unit):
                        nk = 16 if kt == 0 else 128
                        col0 = 0 if kt == 0 else 16 + (kt - 1) * 128
                        S.op("pe", _mm(PS[uu % 3][0:nk, idx * 512:(idx + 1) * 512], KT[:, kv, col0:col0 + nk], R[:, h, :], True, True),
                             reads=[B_KT[kt], B_R[h]], writes=[B_bk[2 * (uu % 3) + idx]])

                def rest(uu, unit, pi):
                    nk = 16 if unit[0] == 0 else 128
                    w = 512 * len(unit)
                    g0 = 16 + 2 * (uu % NP)
                    ptb = R[0:nk, g0:g0 + 2, :].rearrange("p a n -> p (a n)")
                    pbufs = [B_R[g0], B_R[g0 + 1]][0:len(unit)]
                    sbufs = [B_bk[2 * (uu % 3) + i_] for i_ in range(len(unit))]
                    S.op("act", _act(ptb[:, 0:w], PS[uu % 3][0:nk, 0:w], AF.Exp, scale=scale_qk, bias=negc[0:nk, :]),
                         reads=sbufs + [B_negc], writes=pbufs)
                    for idx, kt in enumerate(unit):
                        first = (tc[0] + idx == 0)
                        last = (tc[0] + idx == ntiles - 1)
                        S.op("pe", _mm(bank(bo), V[0:nk, kt, kv * 128:(kv + 1) * 128], ptb[:, idx * 512:(idx + 1) * 512], first, last),
                             reads=[B_V[kt], pbufs[idx]], writes=[B_bk[bo]])
                    tc[0] += len(unit)
                    if unit[0] == 0:
                        S.op("dve", _tt(aD[0:nk, 0:512], aD[0:nk, 0:512], ptb[:, 0:512], ALU.add),
                             reads=pbufs + [bD], writes=[bD])
                        return
                    if pi % PE_EVERY == PE_EVERY - 1:
                        for idx in range(2):
                            S.op("pe", _mm(bank(btot), ones[:, :], ptb[:, idx * 512:(idx + 1) * 512], npe[0] == 0, False),
                                 reads=[B_const, pbufs[idx]], writes=[B_bk[btot]])
                            npe[0] += 1
                        return
                    if ninit["dve"] == 0:
                        S.op("dve", _copy(aD[:, :], ptb[:, :]), reads=pbufs, writes=[bD])
                    else:
                        S.op("dve", _tt(aD[:, :], aD[:, :], ptb[:, :], ALU.add), reads=pbufs + [bD], writes=[bD])
                    ninit["dve"] += 1

                u0 = uctr[0]
                AH = 2
                for u in range(min(AH, len(units))):
                    qk(u0 + u, units[u])
                for u in range(len(units)):
                    if u + AH < len(units):
                        qk(u0 + u + AH, units[u + AH])
                    rest(u0 + u, units[u], u)
                    if u == 1 and pending_fin[0] is not None:
                        fin_b(pending_fin[0])
                        pending_fin[0] = None
                uctr[0] += len(units)
                fin_a(h)
                pending_fin[0] = h
            fin_b(pending_fin[0])
            if G == 0:
                S.dma("sp", _dma(wgrp[:], wgrp_d), "wgrp", reads=[B_wgrpd], writes=[B_wgrp])
            for g in range(4):
                bg = g % 3
                S.op("pe", _mm(bank(bg), wgrp[:, g, :], pooled[g][:, :], True, True),
                     reads=[B_wgrp, B_pooled[g]], writes=[B_bk[bg]])
                S.op("act", _act(R[:, 28 + g, :], bank(bg), AF.Copy, scale=pscale[:, g:g + 1]),
                     reads=[B_bk[bg], B_const], writes=[B_R[28 + g]])
            if G == 0:
                dump_R("b5")
            for j in range(8):
                sm = ring_load(4 + j, 3584)
                M = ring[sm]
                bm_ = B_ring[sm]
                base = 0 if j % 2 == 0 else 4
                bA, bG0, bG1, bPb = base, base + 1, base + 2, base + 3
                for c in range(8):
                    S.op("pe", _mm(bank(bA), M[:, c * 128:(c + 1) * 128], R[:, 8 + c, :], c == 0, c == 7),
                         reads=[bm_, B_R[8 + c]], writes=[B_bk[bA]])
                for c in range(8):
                    S.op("pe", _mm(bank(bG0), M[:, 1024 + c * 128:1024 + (c + 1) * 128], hT[:, c, 8:520], c == 0, c == 7),
                         reads=[bm_] + B_hT, writes=[B_bk[bG0]])
                for c in range(8):
                    S.op("pe", _mm(bank(bG1), M[:, 2048 + c * 128:2048 + (c + 1) * 128], hT[:, c, 8:520], c == 0, c == 7),
                         reads=[bm_] + B_hT, writes=[B_bk[bG1]])
                for g in range(4):
                    S.op("pe", _mm(bank(bPb), M[:, 3072 + g * 128:3072 + (g + 1) * 128], R[:, 28 + g, :], g == 0, g == 3),
                         reads=[bm_, B_R[28 + g]], writes=[B_bk[bPb]])
                t0, t1 = TB[2 * (j % 2)], TB[2 * (j % 2) + 1]
                bt0, bt1 = B_TB[2 * (j % 2)], B_TB[2 * (j % 2) + 1]
                S.op("act", _act(t0[:, 0:512], bank(bG0), AF.Sigmoid), reads=[B_bk[bG0]], writes=[bt0])
                S.op("act", _act(t1[:, 0:512], bank(bG1), AF.Sigmoid), reads=[B_bk[bG1]], writes=[bt1])
                S.op("dve", _tt(t0[:, 0:512], bank(bA), t0[:, 0:512], ALU.mult), reads=[B_bk[bA], bt0], writes=[bt0])
                S.op("dve", _tt(t1[:, 0:512], bank(bPb), t1[:, 0:512], ALU.mult), reads=[B_bk[bPb], bt1], writes=[bt1])
                S.op("pool", _tt(R[:, 20 + j, :], t0[:, 0:512], t1[:, 0:512], ALU.add),
                     reads=[bt0, bt1], writes=[B_R[20 + j]])
            if G == 0:
                dump_R("b6")
            so0 = ring_load(12)
            so1 = ring_load(13)
            wo = [slab3(so0), slab3(so1)]
            wob = [B_ring[so0], B_ring[so1]]

            def Ya(t):
                for hh in range(2):
                    for c in range(8):
                        S.op("pe", _mm(bank(2 * (t % 3) + hh), R[:, 20 + c, t * 128:(t + 1) * 128], wo[hh][:, c, :], c == 0, c == 7),
                             reads=[B_R[20 + c], wob[hh]], writes=[B_bk[2 * (t % 3) + hh]])

            def Yb(t):
                par = t % 2
                ybufs = [B_bk[2 * (t % 3)], B_bk[2 * (t % 3) + 1]]
                S.op("act", _act(qrot[par][:, :], PS[t % 3][:, :], AF.Square, accum_out=ssh[t][:, 0:1]),
                     reads=ybufs, writes=[B_ssh[t], B_qrot[par]])
                S.op("act", _act(sdh[t][:, 0:1], ssh[t][:, 0:1], AF.Sqrt, bias=epsT[:, :], scale=1.0 / D),
                     reads=[B_ssh[t], B_const], writes=[B_sdh[t]])
                S.op("dve", _recip(rstdh[t][:, 0:1], sdh[t][:, 0:1]), reads=[B_sdh[t]], writes=[B_rstdh[t]])

            def Yc(t):
                ybufs = [B_bk[2 * (t % 3)], B_bk[2 * (t % 3) + 1]]
                S.op("dve", _stt(TB[t][:, :], PS[t % 3][:, :], rstdh[t][:, 0:1], gpostmix[:, :], ALU.mult, ALU.mult),
                     reads=ybufs + [B_rstdh[t], B_const], writes=[B_TB[t]])
                S.op("dve", _tt(xres[:, t, :], xres[:, t, :], TB[t][:, :], ALU.add),
                     reads=[B_TB[t], B_x[t]], writes=[B_x[t]])

            modulo([Ya, Yb, Yc, Ha, Ha2, mk_Hb(gcolmlp)], 4)
            if G == 0 and dbg:
                S.dma("sp", _dma(dbg_x1, xres[:]), "dbg", reads=B_x, writes=[B_dbg])
            for s_ in range(8):
                si = ring_load(14 + s_)
                wi = slab3(si)
                for fl in range(4):
                    f = s_ * 4 + fl
                    bu = f % 4
                    for c in range(8):
                        S.op("pe", _mm(bank(bu), wi[:, c, fl * 128:(fl + 1) * 128], hT[:, c, 8:520], c == 0, c == 7),
                             reads=[B_ring[si]] + B_hT, writes=[B_bk[bu]])
                    tb = TB[bu]
                    S.op("act", _act(tb[:, 0:512], bank(bu), AF.Relu), reads=[B_bk[bu]], writes=[B_TB[bu]])
                    S.op("dve" if f % 4 != 3 else "pool", _tt(R[:, f, :], tb[:, 0:512], tb[:, 0:512], ALU.mult),
                         reads=[B_TB[bu]], writes=[B_R[f]])
            def Za(t):
                par = t % 2
                zb = [B_bk[2 * t], B_bk[2 * t + 1]]
                S.op("act", _act(qrot[par][:, :], PS[t][:, :], AF.Square, accum_out=ssh[t][:, 0:1]),
                     reads=zb, writes=[B_ssh[t], B_qrot[par]])
                S.op("act", _act(sdh[t][:, 0:1], ssh[t][:, 0:1], AF.Sqrt, bias=epsT[:, :], scale=1.0 / D),
                     reads=[B_ssh[t], B_const], writes=[B_sdh[t]])
                S.op("dve", _recip(rstdh[t][:, 0:1], sdh[t][:, 0:1]), reads=[B_sdh[t]], writes=[B_rstdh[t]])

            def Zb(t):
                zb = [B_bk[2 * t], B_bk[2 * t + 1]]
                S.op("dve", _stt(TB[t][:, :], PS[t][:, :], rstdh[t][:, 0:1], gpostmlp[:, :], ALU.mult, ALU.mult),
                     reads=zb + [B_rstdh[t], B_const], writes=[B_TB[t]])
                S.op("pool", _tt(TB[t][:, :], TB[t][:, :], xres[:, t, :], ALU.add),
                     reads=[B_TB[t], B_x[t]], writes=[B_TB[t]])
                S.dma("pool", _dma(out[r0 + t * 128:r0 + (t + 1) * 128, :], TB[t][:, :]), f"ost{t}",
                      reads=[B_TB[t]], writes=[B_out[t]])

            for hh in range(2):
                for blk in range(4):
                    so = ring_load(22 + hh * 4 + blk)
                    wmo_ = slab3(so)
                    last_slab = (hh == 1 and blk == 3)
                    order = ([(cc, t) for t in range(4) for cc in range(8)] if last_slab
                             else [(cc, t) for cc in range(8) for t in range(4)])
                    for (cc, t) in order:
                        f = blk * 8 + cc
                        S.op("pe", _mm(bank(2 * t + hh), R[:, f, t * 128:(t + 1) * 128], wmo_[:, cc, :], f == 0, f == 31),
                             reads=[B_R[f], B_ring[so]], writes=[B_bk[2 * t + hh]])
                        if last_slab and cc == 7:
                            Za(t)
                            if t >= 1:
                                Zb(t - 1)
            Zb(3)

        S.wait_all("sp", B_out + [B_dbg])

        eng_sems = {}
        for e in ENGS:
            for ep in range(S.n_epochs):
                eng_sems[(e, ep)] = st.enter_context(nc.semaphore(f"s_{e}_{ep}"))
        dma_sems = {k: st.enter_context(nc.semaphore(f"d_{k}")) for k in S.dma_keys()}
        block = st.enter_context(nc.Block())
        stats = S.emit(block, eng_sems, dma_sems)
        build_program.last_stats = stats
    return nc


def _rope_tables(real_idx):
    quarter = 32
    inv_freq = (10000.0 ** (-np.arange(quarter, dtype=np.float32) / np.float32(quarter))).astype(np.float32)
    idx = np.asarray(real_idx)
    valid = idx >= 0
    rows = np.where(valid, idx // 64, 0).astype(np.float32)
    cols = np.where(valid, idx % 64, 0).astype(np.float32)
    ang_r = (rows[:, None] * inv_freq[None, :]).astype(np.float32)
    ang_c = (cols[:, None] * inv_freq[None, :]).astype(np.float32)
    cr, sr, cc, sc = np.cos(ang_r), np.sin(ang_r), np.cos(ang_c), np.sin(ang_c)
    C = np.concatenate([cr, cr, cc, cc], axis=1)
    Sn = np.concatenate([-sr, sr, -sc, sc], axis=1)
    return np.concatenate([C, Sn], axis=1).astype(np.float32)


def _swap32(g):
    g4 = np.asarray(g, np.float32).reshape(2, 2, 32)
    return np.ascontiguousarray(g4[:, ::-1, :]).reshape(128)


def make_in_maps(x, meta_tokens, pre_mix_g, q_norm_g, k_norm_g, w_in, w_attn_br, w_pool_grp, pool_scale,
                 w_pool_br, w_out, post_mix_g, pre_mlp_g, w_mlp_in, w_mlp_out, post_mlp_g):
    f = np.float32
    x = np.asarray(x, f)
    meta = np.asarray(meta_tokens, f)
    shared = {
        "w_in": np.ascontiguousarray(np.asarray(w_in, f)[0]),
        "w_attn_br": np.ascontiguousarray(np.asarray(w_attn_br, f)[0]),
        "w_pool_grp": np.ascontiguousarray(np.asarray(w_pool_grp, f)[0]),
        "w_pool_br": np.ascontiguousarray(np.asarray(w_pool_br, f)[0]),
        "w_out": np.ascontiguousarray(np.asarray(w_out, f)[0]),
        "w_mlp_in": np.ascontiguousarray(np.asarray(w_mlp_in, f)[0]),
        "w_mlp_out": np.ascontiguousarray(np.asarray(w_mlp_out, f)[0]),
        "g_premix_c": np.ascontiguousarray(np.asarray(pre_mix_g, f)[0].reshape(8, 128).T),
        "g_premlp_c": np.ascontiguousarray(np.asarray(pre_mlp_g, f)[0].reshape(8, 128).T),
        "g_pscale_c": np.ascontiguousarray(np.asarray(pool_scale, f)[0].reshape(4, 128).T),
        "g_postmix": np.ascontiguousarray(np.asarray(post_mix_g, f)[0]),
        "g_postmlp": np.ascontiguousarray(np.asarray(post_mlp_g, f)[0]),
        "g_q": np.ascontiguousarray(np.asarray(q_norm_g, f)[0]),
        "g_qs": _swap32(np.asarray(q_norm_g, f)[0]),
        "g_k": np.ascontiguousarray(np.asarray(k_norm_g, f)[0]),
        "g_ks": _swap32(np.asarray(k_norm_g, f)[0]),
        "ident": np.eye(128, dtype=f).astype(ml_dtypes.bfloat16),
    }
    kidx = np.concatenate([np.full(128, -1), np.arange(SEQ)])
    ropeK = _rope_tables(kidx).reshape(NKT, 128, 256)
    shared["ropeK"] = ropeK
    in_maps = []
    for core in range(8):
        b, hf = core // 2, core % 2
        xkv = np.zeros((NKT * 128, D), f)
        xkv[0:N_META] = meta
        xkv[128:] = x[b]
        q0 = hf * NQ
        xq = np.zeros((NQ + 16, D), f)
        xq[8:8 + NQ] = x[b, q0:q0 + NQ]
        if hf == 0:
            xq[0:8] = meta[8:16]
            xq[8 + NQ:] = x[b, NQ:NQ + 8]
        else:
            xq[0:8] = x[b, q0 - 8:q0]
        ropeQ = _rope_tables(np.arange(q0, q0 + NQ)).reshape(NQ // 128, 128, 256)
        inv = np.zeros((NG, 4, 16), f)
        for G in range(NG):
            for g in range(4):
                w_ = 2 << g
                treal = q0 + G * GT + GT - 16 + np.arange(16)
                tpos = treal + N_META
                lo = np.clip(tpos - w_ // 2, 0, LTOT)
                hi = np.clip(tpos + w_ - w_ // 2, 0, LTOT)
                inv[G, g] = 1.0 / (hi - lo).astype(f)
        m = dict(shared)
        m["xkv"] = xkv
        m["xq"] = xq
        m["ropeQ"] = ropeQ
        m["invtail"] = inv.reshape(-1)
        in_maps.append(m)
    return in_maps


_NC_CACHE = {}


def kernel(**inputs):
    in_maps = make_in_maps(**inputs)
    if "nc" not in _NC_CACHE:
        _NC_CACHE["nc"] = build_program()
    nc = _NC_CACHE["nc"]
    res = run_bass_kernel_spmd(nc, in_maps, core_ids=list(range(8)))
    outp = np.empty((BATCH, SEQ, D), np.float32)
    for core in range(8):
        b, hf = core // 2, core % 2
        outp[b, hf * NQ:(hf + 1) * NQ] = np.asarray(res.results[core]["out"], np.float32)
    return outp
```
